# Optimizing a Trainium2 kernel written in Bass

```python
import math
import jax, jax.numpy as jnp
from jax import lax
import numpy as np

D_MODEL = 2048
BATCH = 4
SEQ = 2048
DEPTH = 2
DEC_BATCH = 1
DEC_SEQ = 16384
PAST_LEN = 128

ATT_HEADS = 16
ATT_HEAD_DIM = 128
ATT_WIDTH = ATT_HEADS * ATT_HEAD_DIM
DILATED_PATTERNS = ((128, 1), (512, 4), (2048, 16))
POOL_WINDOWS = (2, 4, 8, 16)
POOL_WIDTH = D_MODEL // 2
POOL_GROUP = POOL_WIDTH // len(POOL_WINDOWS)
CONV_WIDTH = D_MODEL
CONV_TAPS = 3
REL_BUCKETS = 32
REL_MAX_DISTANCE = 1024
DN_ALPHA = (2 * DEPTH) ** 0.25
DN_BETA = (8 * DEPTH) ** -0.25
LN_EPS = 1e-5
NEG_INF = -1e30

N_EVEN = (DEPTH + 1) // 2
N_ODD = DEPTH // 2
IN_AB = 4 * ATT_WIDTH + 2 * POOL_WIDTH
IN_C = 4 * CONV_WIDTH

kernel_name = "hybrid_dilated_pool_shortconv_encoder"


def _t5_bucket(rel):
    nb = REL_BUCKETS // 2
    max_exact = nb // 2
    ret = np.where(rel > 0, nb, 0)
    n = np.abs(rel)
    n_safe = np.maximum(n, 1).astype(np.float64)
    large = max_exact + (np.log(n_safe / max_exact) / math.log(REL_MAX_DISTANCE / max_exact)
                         * (nb - max_exact)).astype(np.int64)
    large = np.minimum(large, nb - 1)
    return (ret + np.where(n < max_exact, n, large)).astype(np.int32)


def _layernorm(x, g, b):
    xf = x.astype(jnp.float32)
    mu = jnp.mean(xf, axis=-1, keepdims=True)
    var = jnp.mean(jnp.square(xf - mu), axis=-1, keepdims=True)
    return ((xf - mu) * lax.rsqrt(var + LN_EPS) * g + b).astype(x.dtype)


def _dilated_pattern(q, k, v, rel_bias, window, dilation):
    b, s, h, hd = q.shape
    r = window // (2 * dilation)
    L = s // dilation
    nb = -(-L // r)
    Lp = nb * r

    def to_strided(t):
        t = t.reshape(b, L, dilation, h, hd).transpose(0, 2, 1, 3, 4)
        return jnp.pad(t, ((0, 0), (0, 0), (0, Lp - L), (0, 0), (0, 0)))

    def band(t):
        tp = jnp.pad(t, ((0, 0), (0, 0), (r, r), (0, 0), (0, 0))).reshape(b, dilation, nb + 2, r, h, hd)
        return jnp.concatenate([tp[:, :, :-2], tp[:, :, 1:-1], tp[:, :, 2:]], axis=3)

    qb = to_strided(q).reshape(b, dilation, nb, r, h, hd)
    kb = band(to_strided(k))
    vb = band(to_strided(v))

    rel = np.arange(3 * r)[None, :] - r - np.arange(r)[:, None]
    in_band = np.abs(rel) <= r
    kpos = np.arange(nb)[:, None] * r - r + np.arange(3 * r)[None, :]
    valid = (kpos >= 0) & (kpos < L)
    mask = jnp.asarray(in_band[None, :, :] & valid[:, None, :])
    bias = rel_bias[_t5_bucket(rel * dilation)].astype(jnp.float32).transpose(2, 0, 1)

    scores = jnp.einsum('bgnqhd,bgnkhd->bgnhqk', qb, kb,
                        preferred_element_type=jnp.float32) * (ATT_HEAD_DIM ** -0.5)
    scores = jnp.where(mask[None, None, :, None], scores + bias[None, None, None], NEG_INF)
    m = jnp.max(scores, axis=-1, keepdims=True)
    p = jnp.exp(scores - m)
    den = jnp.sum(p, axis=-1)
    o = jnp.einsum('bgnhqk,bgnkhd->bgnqhd', p.astype(v.dtype), vb,
                   preferred_element_type=jnp.float32)
    o = o / den.transpose(0, 1, 2, 4, 3)[..., None]
    lse = m[..., 0] + jnp.log(den)

    o = o.reshape(b, dilation, Lp, h, hd)[:, :, :L].transpose(0, 2, 1, 3, 4).reshape(b, s, h, hd)
    lse = lse.transpose(0, 1, 2, 4, 3).reshape(b, dilation, Lp, h)[:, :, :L]
    lse = lse.transpose(0, 2, 1, 3).reshape(b, s, h)
    return o, lse


def _dilated_attention(q, k, v, rel_bias):
    results = [_dilated_pattern(q, k, v, rel_bias, w, d) for (w, d) in DILATED_PATTERNS]
    outs = jnp.stack([o for o, _ in results], axis=0)
    lses = jnp.stack([l for _, l in results], axis=0)
    wts = jax.nn.softmax(lses, axis=0)
    return jnp.einsum('pbsh,pbshd->bshd', wts, outs)


def _multiscale_pool_minus_self(u):
    b, s, g, c = u.shape
    uf = u.astype(jnp.float32)
    cs = jnp.pad(jnp.cumsum(uf, axis=1), ((0, 0), (1, 0), (0, 0), (0, 0)))
    pos = np.arange(s)
    outs = []
    for gi, w in enumerate(POOL_WINDOWS):
        lo = np.maximum(pos - w // 2, 0)
        hi = np.minimum(pos + w // 2 - 1, s - 1)
        cnt = jnp.asarray((hi - lo + 1).astype(np.float32))
        csg = cs[:, :, gi]
        mean = (csg[:, hi + 1] - csg[:, lo]) / cnt[None, :, None]
        outs.append(mean - uf[:, :, gi])
    return jnp.stack(outs, axis=2)


def _dwconv3(u, w):
    up = jnp.pad(u, ((0, 0), (1, 1), (0, 0)))
    return up[:, :-2] * w[0] + up[:, 1:-1] * w[1] + up[:, 2:] * w[2]


def _even_layer(x, rel_bias, w_in, pool_w, pool_scale, w_out):
    b, s, _ = x.shape
    hproj = x @ w_in
    q, k, v, g_a, u_b, g_b = jnp.split(
        hproj, [ATT_WIDTH, 2 * ATT_WIDTH, 3 * ATT_WIDTH, 4 * ATT_WIDTH, 4 * ATT_WIDTH + POOL_WIDTH], axis=-1)
    shp = (b, s, ATT_HEADS, ATT_HEAD_DIM)
    o_a = _dilated_attention(q.reshape(shp), k.reshape(shp), v.reshape(shp), rel_bias)
    o_a = o_a.reshape(b, s, ATT_WIDTH).astype(x.dtype) * jax.nn.silu(g_a)
    pooled = _multiscale_pool_minus_self(u_b.reshape(b, s, len(POOL_WINDOWS), POOL_GROUP))
    o_b = jnp.einsum('bsgc,gcd->bsgd', pooled.astype(x.dtype), pool_w).reshape(b, s, POOL_WIDTH)
    o_b = o_b * pool_scale * jax.nn.silu(g_b)
    return jnp.concatenate([o_a, o_b], axis=-1) @ w_out


def _odd_layer(x, w_in, conv_w, w_out):
    hproj = x @ w_in
    g_b, g_c, val, gate = jnp.split(hproj, [CONV_WIDTH, 2 * CONV_WIDTH, 3 * CONV_WIDTH], axis=-1)
    y = g_b * _dwconv3(g_c * val, conv_w) * jax.nn.silu(gate)
    return y @ w_out


def _trunk(x, rel_bias, w_in_ab, pool_w, pool_scale, w_out_ab, w_in_c, conv_w, w_out_c, ln_g, ln_b):
    for layer in range(DEPTH):
        i = layer // 2
        if layer % 2 == 0:
            f = _even_layer(x, rel_bias, w_in_ab[i], pool_w[i], pool_scale[i], w_out_ab[i])
        else:
            f = _odd_layer(x, w_in_c[i], conv_w[i], w_out_c[i])
        x = _layernorm(DN_ALPHA * x + f, ln_g[layer], ln_b[layer])
    return x


def setup_inputs(seed: int = 0) -> dict:
    key = jax.random.key(seed)
    ks = jax.random.split(key, 12)
    f32 = jnp.float32
    nrm = lambda k, shape, scale: jax.random.normal(k, shape, f32) * scale
    return {
        "x_prompt": nrm(ks[0], (BATCH, SEQ, D_MODEL), 1.0),
        "x_sample": nrm(ks[1], (DEC_BATCH, DEC_SEQ, D_MODEL), 1.0),
        "rel_bias": nrm(ks[2], (REL_BUCKETS, ATT_HEADS), 0.2),
        "w_in_ab": nrm(ks[3], (N_EVEN, D_MODEL, IN_AB), D_MODEL ** -0.5),
        "pool_w": nrm(ks[4], (N_EVEN, len(POOL_WINDOWS), POOL_GROUP, POOL_GROUP), POOL_GROUP ** -0.5),
        "pool_scale": 1.0 + nrm(ks[5], (N_EVEN, POOL_WIDTH), 0.1),
        "w_out_ab": nrm(ks[6], (N_EVEN, ATT_WIDTH + POOL_WIDTH, D_MODEL),
                         DN_BETA * (ATT_WIDTH + POOL_WIDTH) ** -0.5),
        "w_in_c": nrm(ks[7], (N_ODD, D_MODEL, IN_C), D_MODEL ** -0.5),
        "conv_w": nrm(ks[8], (N_ODD, CONV_TAPS, CONV_WIDTH), CONV_TAPS ** -0.5),
        "w_out_c": nrm(ks[9], (N_ODD, CONV_WIDTH, D_MODEL), DN_BETA * CONV_WIDTH ** -0.5),
        "ln_g": 1.0 + nrm(ks[10], (DEPTH, D_MODEL), 0.02),
        "ln_b": nrm(ks[11], (DEPTH, D_MODEL), 0.02),
    }


def reference(x_prompt, x_sample, rel_bias, w_in_ab, pool_w, pool_scale, w_out_ab,
              w_in_c, conv_w, w_out_c, ln_g, ln_b):
    y_prompt = _trunk(x_prompt, rel_bias, w_in_ab, pool_w, pool_scale, w_out_ab,
                      w_in_c, conv_w, w_out_c, ln_g, ln_b)
    y_sample = _trunk(x_sample, rel_bias, w_in_ab, pool_w, pool_scale, w_out_ab,
                      w_in_c, conv_w, w_out_c, ln_g, ln_b)
    return (y_prompt, y_sample)
```

```python
import math
from contextlib import ExitStack, contextmanager

import numpy as np
import concourse.bass as bass
import concourse.mybir as mybir
from concourse.bass_utils import run_bass_kernel_spmd

F32 = mybir.dt.float32
BF16 = mybir.dt.bfloat16
AF = mybir.ActivationFunctionType
ALU = mybir.AluOpType
AX = mybir.AxisListType

D = 2048
NH = 16
HD = 128
NCORES = 8
S_SAMPLE = 16384
S_PROMPT = 2048
ALPHA = 4.0 ** 0.25
LN_EPS = 1e-5
SCALE = HD ** -0.5
POOL_WINDOWS = (2, 4, 8, 16)

E = 128
WOFF = 1024
NKT = 17
NV12 = 768
NV = 1152
NEG = -30000.0


class Seg:
    def __init__(self, n_own):
        self.nOwn = n_own
        self.nQ = n_own + 2 * E
        self.W = self.nQ + 2048
        self.T = self.W // 128
        self.nqt = self.nQ // 128
        self.nD = n_own + 2


SEGS = [Seg(2048), Seg(1024)]
PASSES = [(0, 0, 2176), (0, 2176, 4352), (1, 0, 1664), (1, 1664, 3328)]
XT_MAX = 2176


def split_blocks(n, maxn=512, gran=64):
    k = -(-n // maxn)
    base = -(-n // k)
    base = -(-base // gran) * gran
    out = []
    off = 0
    while off < n:
        m = min(base, n - off)
        out.append((off, m))
        off += m
    return out


class Res:
    __slots__ = ("name", "w", "r")

    def __init__(self, name=""):
        self.name = name
        self.w = {}
        self.r = {}


class _Eng:
    def __init__(self, name, h, sem):
        self.name, self.h, self.sem, self.n, self.waited = name, h, sem, 0, {}


class DSem:
    def __init__(self, sem):
        self.sem, self.n = sem, 0


class Emitter:
    def __init__(self, nc, stack):
        self.nc = nc
        self.E = {}
        for name, h in (("pe", nc.tensor), ("act", nc.scalar), ("dve", nc.vector),
                        ("pool", nc.gpsimd), ("sp", nc.sync)):
            sem = stack.enter_context(nc.semaphore("s_" + name))
            self.E[name] = _Eng(name, h, sem)
        self._stack = stack
        self._nd = 0
        self._all_ds = []

    def dsem(self, stack=None):
        self._nd += 1
        sem = self._stack.enter_context(self.nc.semaphore("d%d" % self._nd))
        d = DSem(sem)
        self._all_ds.append(d)
        return d

    def barrier(self):
        for e in self.E.values():
            for o in self.E.values():
                if o is e or o.n == 0:
                    continue
                k = id(o.sem)
                if e.waited.get(k, 0) < o.n:
                    e.h.wait_ge(o.sem, o.n)
                    e.waited[k] = o.n
            for d in self._all_ds:
                k = id(d.sem)
                if d.n and e.waited.get(k, 0) < d.n:
                    e.h.wait_ge(d.sem, d.n)
                    e.waited[k] = d.n

    def _wait(self, e, reads, writes):
        need = {}
        for r in reads:
            for k, sv in r.w.items():
                if need.get(k, (None, 0))[1] < sv[1]:
                    need[k] = sv
        for w in writes:
            for dd in (w.w, w.r):
                for k, sv in dd.items():
                    if need.get(k, (None, 0))[1] < sv[1]:
                        need[k] = sv
        own = id(e.sem)
        for k, (s, v) in need.items():
            if k == own and e.name == "pe":
                continue
            if e.waited.get(k, 0) < v:
                e.h.wait_ge(s, v)
                e.waited[k] = v

    def _commit(self, ev, reads, writes):
        k = id(ev[0])
        for r in reads:
            if r.r.get(k, (None, 0))[1] < ev[1]:
                r.r[k] = ev
        for w in writes:
            w.w = {k: ev}
            w.r = {}

    def op(self, eng, fn, reads=(), writes=()):
        e = self.E[eng]
        self._wait(e, reads, writes)
        ins = fn(e.h)
        e.n += 1
        ins.then_inc(e.sem, 1)
        self._commit((e.sem, e.n), reads, writes)

    def group(self, eng, fns, reads=(), writes=()):
        e = self.E[eng]
        self._wait(e, reads, writes)
        ins = None
        for fn in fns:
            ins = fn(e.h)
        e.n += 1
        ins.then_inc(e.sem, 1)
        self._commit((e.sem, e.n), reads, writes)

    def dma(self, q, pairs, ds, reads=(), writes=(), accum=False, **kw):
        e = self.E[q]
        if accum:
            saved = [(w, w.w) for w in writes]
            for w in writes:
                w.w = {}
            self._wait(e, reads, writes)
            for w, ww in saved:
                w.w = ww
        else:
            self._wait(e, reads, writes)
        for (o, i) in pairs:
            e.h.dma_start(out=o, in_=i, **kw).then_inc(ds.sem, 16)
            ds.n += 16
        ev = (ds.sem, ds.n)
        if accum:
            k = id(ds.sem)
            for r in reads:
                if r.r.get(k, (None, 0))[1] < ev[1]:
                    r.r[k] = ev
            for w in writes:
                if w.r:
                    w.w = {}
                    w.r = {}
                w.w[k] = ev
        else:
            self._commit(ev, reads, writes)

    def finish(self, resources):
        self._wait(self.E["sp"], resources, ())


class Ring:
    def __init__(self, em, stack, nc, name, n, shape, dtype, psum=False, dsem=False):
        self.bufs, self.res, self.ds = [], [], []
        for i in range(n):
            if psum:
                t = stack.enter_context(nc.psum_tensor("%s%d" % (name, i), list(shape), dtype))
            else:
                t = stack.enter_context(nc.sbuf_tensor("%s%d" % (name, i), list(shape), dtype))
            self.bufs.append(t)
            self.res.append(Res("%s%d" % (name, i)))
            self.ds.append(em.dsem(stack) if dsem else None)
        self.i = 0
        self.n = n

    def next(self):
        j = self.i % self.n
        self.i += 1
        return self.bufs[j], self.res[j], self.ds[j]


def _t5_bucket(rel):
    nb = 16
    max_exact = 8
    ret = np.where(rel > 0, nb, 0)
    n = np.abs(rel)
    n_safe = np.maximum(n, 1).astype(np.float64)
    large = max_exact + (np.log(n_safe / max_exact) / math.log(1024 / max_exact) * (nb - max_exact)).astype(np.int64)
    large = np.minimum(large, nb - 1)
    return (ret + np.where(n < max_exact, n, large)).astype(np.int32)


def _onehot_table():
    oh = np.zeros((33, NV), np.float32)
    n = np.arange(NV12)
    d = n - 383
    ad = np.abs(d)
    mult = (ad <= 64).astype(np.int64) + ((d % 4 == 0) & (ad <= 256))
    oh[_t5_bucket(d), n] = 1.0
    oh[32, :NV12] = np.where(mult > 0, np.log(np.maximum(mult, 1)), NEG).astype(np.float32)
    n3 = np.arange(NV - NV12)
    dl = n3 - 191
    oh[_t5_bucket(16 * dl), NV12 + n3] = 1.0
    oh[32, NV12:] = np.where(np.abs(dl) <= 64, 0.0, NEG).astype(np.float32)
    return oh


def build_program(debug=False, phases="ABCDE"):
    nc = bass.Bass("TRN2", target_bir_lowering=False)
    dk = "ExternalOutput" if debug else "Internal"

    def din(name, shape, dt=F32):
        return nc.dram_tensor(name, list(shape), dt, kind="ExternalInput").ap()

    def dscr(name, shape, dt):
        return nc.dram_tensor(name, list(shape), dt, kind=dk).ap()

    xw = [din("xw%d" % s, [SEGS[s].W, D]) for s in range(2)]
    vld = [din("vld%d" % s, [SEGS[s].W]) for s in range(2)]
    icnt = [din("icnt%d" % s, [4, SEGS[s].nQ]) for s in range(2)]
    flags = din("flags", [128, 4])
    ident_d = din("ident", [128, 128])
    ohv_d = din("ohv", [33, NV])
    rba_d = din("rba", [33, NH])
    w_in_ab = din("w_in_ab", [D, 10240])
    pool_w = din("pool_w", [4, 256, 256])
    pscale_d = din("pscale", [128, 8])
    w_out_ab = din("w_out_ab", [3072, D])
    w_in_c = din("w_in_c", [D, 8192])
    convw_d = din("convw", [128, 16, 3])
    w_out_c = din("w_out_c", [D, D])
    ln_g = din("ln_g", [2, D])
    ln_b = din("ln_b", [2, D])

    yout = [nc.dram_tensor("y%d" % s, [SEGS[s].nOwn, D], F32, kind="ExternalOutput").ap() for s in range(2)]

    KT = [dscr("KT%d" % s, [D, SEGS[s].W], BF16) for s in range(2)]
    VS = [dscr("VS%d" % s, [SEGS[s].W, D], BF16) for s in range(2)]
    QT = [dscr("QT%d" % s, [D, SEGS[s].nQ], BF16) for s in range(2)]
    GA = [dscr("GA%d" % s, [D, SEGS[s].nQ], BF16) for s in range(2)]
    UB = [dscr("UB%d" % s, [1024, SEGS[s].nQ], F32) for s in range(2)]
    GB = [dscr("GB%d" % s, [1024, SEGS[s].nQ], BF16) for s in range(2)]
    OG = [dscr("OG%d" % s, [3072, SEGS[s].nQ], BF16) for s in range(2)]
    X1 = [dscr("X1_%d" % s, [SEGS[s].nD, D], F32) for s in range(2)]
    X1T = [dscr("X1T%d" % s, [D, SEGS[s].nD], BF16) for s in range(2)]
    YT = [dscr("YT%d" % s, [D, SEGS[s].nOwn], BF16) for s in range(2)]
    VECR = nc.dram_tensor("VECR", [NH, NV], BF16, kind=dk)

    R_KT = [Res() for _ in range(2)]
    R_VS = [Res() for _ in range(2)]
    R_QT = [Res() for _ in range(2)]
    R_GA = [Res() for _ in range(2)]
    R_UB = [Res() for _ in range(2)]
    R_GB = [Res() for _ in range(2)]
    R_OG = [Res() for _ in range(2)]
    R_X1 = [Res() for _ in range(2)]
    R_X1T = [Res() for _ in range(2)]
    R_YT = [Res() for _ in range(2)]
    R_VECR = Res()
    R_Y = [Res() for _ in range(2)]

    with ExitStack() as top:
        em = Emitter(nc, top)

        @contextmanager
        def phase():
            with ExitStack() as _ph:
                yield _ph
                em.barrier()

        ident = top.enter_context(nc.sbuf_tensor("identb", [128, 128], BF16))
        R_ident = Res()
        d_const = em.dsem()
        em.dma("pool", [(ident[:], ident_d)], d_const, writes=[R_ident])

        if "A" in phases or "B" in phases:
            with phase() as ph:
                ohv = ph.enter_context(nc.sbuf_tensor("ohvb", [33, NV], F32))
                rba = ph.enter_context(nc.sbuf_tensor("rbab", [33, NH], F32))
                vst = ph.enter_context(nc.sbuf_tensor("vecst", [NH, NV], BF16))
                pv = ph.enter_context(nc.psum_tensor("pvec", [128, 512], F32))
                R_ohv, R_rba, R_vst, R_pv = Res(), Res(), Res(), Res()
                dd = em.dsem(ph)
                em.dma("sp", [(ohv[:], ohv_d), (rba[:], rba_d)], dd, writes=[R_ohv, R_rba])
                for c0 in range(0, NV, 512):
                    n = min(512, NV - c0)
                    em.op("pe", lambda e, c0=c0, n=n: e.matmul(pv[0:NH, 0:n], rba[:, :], ohv[:, c0:c0 + n], start=True, stop=True),
                          reads=[R_ohv, R_rba], writes=[R_pv])
                    em.op("act", lambda e, c0=c0, n=n: e.activation(vst[:, c0:c0 + n], pv[0:NH, 0:n], AF.Exp),
                          reads=[R_pv], writes=[R_vst])
                em.dma("sp", [(VECR.ap(), vst[:])], dd, reads=[R_vst], writes=[R_VECR])
                em.barrier()

        if "A" in phases:
            with phase() as ph:
                xT = [ph.enter_context(nc.sbuf_tensor("xT%d" % i, [128, 16, XT_MAX], BF16)) for i in range(2)]
                R_xT = [[Res() for _ in range(17)] for _ in range(2)]
                xb = Ring(em, ph, nc, "xb", 2, [128, D], BF16, dsem=True)
                slab = Ring(em, ph, nc, "slab", 4, [128, 16, 256], BF16, dsem=True)
                st16 = Ring(em, ph, nc, "st16", 3, [128, XT_MAX], BF16, dsem=True)
                st32 = Ring(em, ph, nc, "st32", 2, [128, 1216], F32, dsem=True)
                stv = Ring(em, ph, nc, "stv", 4, [128, 256], BF16, dsem=True)
                mm = Ring(em, ph, nc, "mmA", 4, [128, 512], F32, psum=True)
                tr = Ring(em, ph, nc, "trA", 2, [128, 8, 128], BF16, psum=True)
                evi = [0]

                def evac(out_ap, in_ap, reads, writes):
                    evi[0] += 1
                    if evi[0] % 2:
                        em.op("act", lambda e: e.activation(out_ap, in_ap, AF.Copy), reads=reads, writes=writes)
                    else:
                        em.op("dve", lambda e: e.tensor_copy(out_ap, in_ap), reads=reads, writes=writes)

                def build_tile(p, t):
                    seg, w0, w1 = PASSES[p]
                    b, rb, db = xb.next()
                    em.dma("pool", [(b[:], xw[seg][w0 + 128 * t: w0 + 128 * t + 128, :])], db, writes=[rb])
                    for g in range(2):
                        pt, rpt, _ = tr.next()
                        em.group("pe", [(lambda e, c=c, pt=pt: e.transpose(pt[:, c % 8, :], b[:, c * 128:(c + 1) * 128], ident[:]))
                                        for c in range(8 * g, 8 * g + 8)], reads=[rb, R_ident], writes=[rpt])
                        evac(xT[p % 2][:, 8 * g:8 * g + 8, 128 * t:128 * t + 128], pt[:], [rpt], [R_xT[p % 2][t]])

                def slab_list(p):
                    seg, w0, w1 = PASSES[p]
                    sg = SEGS[seg]
                    q0 = max(w0, WOFF)
                    q1 = min(w1, WOFF + sg.nQ)
                    L = []
                    for j in range(8):
                        L.append(("K", 2048 + 256 * j, j))
                    for j in range(8):
                        L.append(("V", 4096 + 256 * j, j))
                    if q1 > q0:
                        for j in range(8):
                            L.append(("Q", 256 * j, j))
                        for j in range(8):
                            L.append(("GA", 6144 + 256 * j, j))
                        for j in range(4):
                            L.append(("UB", 8192 + 256 * j, j))
                        for j in range(4):
                            L.append(("GB", 9216 + 256 * j, j))
                    return L

                all_slabs = []
                for p in range(len(PASSES)):
                    for it in slab_list(p):
                        all_slabs.append((p,) + it)
                loaded = {}

                def load_slab(i):
                    if i >= len(all_slabs) or i in loaded:
                        return
                    p, kind, c0, j = all_slabs[i]
                    b, rb, db = slab.next()
                    em.dma("pool", [(b[:], w_in_ab[:, c0:c0 + 256].rearrange("(k p) n -> p k n", p=128))], db, writes=[rb])
                    loaded[i] = (b, rb)

                for t in range((PASSES[0][2] - PASSES[0][1]) // 128):
                    build_tile(0, t)
                load_slab(0)
                load_slab(1)
                si = 0
                for p in range(len(PASSES)):
                    seg, w0, w1 = PASSES[p]
                    sg = SEGS[seg]
                    ntile = (w1 - w0) // 128
                    q0 = max(w0, WOFF)
                    q1 = min(w1, WOFF + sg.nQ)
                    xt = xT[p % 2]
                    rxt = R_xT[p % 2]
                    nxt_tiles = []
                    if p + 1 < len(PASSES):
                        nxt_tiles = list(range((PASSES[p + 1][2] - PASSES[p + 1][1]) // 128))
                    sl = slab_list(p)
                    for li, (kind, c0, j) in enumerate(sl):
                        load_slab(si + 2)
                        b, rb = loaded.pop(si)
                        si += 1
                        if kind == "V":
                            for t in range(ntile):
                                bank, rbank, _ = mm.next()
                                em.group("pe", [(lambda e, k=k, t=t, bank=bank, b=b: e.matmul(bank[:, 0:256], xt[:, k, 128 * t:128 * t + 128], b[:, k, :],
                                                                                 start=(k == 0), stop=(k == 15))) for k in range(16)],
                                         reads=[rxt[t], rb], writes=[rbank])
                                sv, rsv, dsv = stv.next()
                                evac(sv[:], bank[:, 0:256], [rbank], [rsv])
                                em.dma("sp", [(VS[seg][w0 + 128 * t:w0 + 128 * t + 128, 256 * j:256 * j + 256], sv[:])], dsv,
                                       reads=[rsv], writes=[R_VS[seg]], accum=True)
                        else:
                            if kind == "K":
                                t0, t1 = w0, w1
                            else:
                                t0, t1 = q0, q1
                            ntok = t1 - t0
                            blocks = split_blocks(ntok)
                            for mc in range(2):
                                if kind == "UB":
                                    sb, rsb, dsb = st32.next()
                                else:
                                    sb, rsb, dsb = st16.next()
                                for (off, n) in blocks:
                                    lo = t0 - w0 + off
                                    tl = list(range(lo // 128, (lo + n - 1) // 128 + 1))
                                    bank, rbank, _ = mm.next()
                                    em.group("pe", [(lambda e, k=k, bank=bank, b=b, lo=lo, n=n, mc=mc: e.matmul(bank[:, 0:n], b[:, k, 128 * mc:128 * mc + 128], xt[:, k, lo:lo + n],
                                                                                                  start=(k == 0), stop=(k == 15))) for k in range(16)],
                                             reads=[rxt[t] for t in tl] + [rb], writes=[rbank])
                                    evac(sb[:, off:off + n], bank[:, 0:n], [rbank], [rsb])
                                row = 256 * j + 128 * mc
                                if kind == "K":
                                    dst, rdst = KT[seg][row:row + 128, t0:t1], R_KT[seg]
                                elif kind == "Q":
                                    dst, rdst = QT[seg][row:row + 128, t0 - WOFF:t1 - WOFF], R_QT[seg]
                                elif kind == "GA":
                                    dst, rdst = GA[seg][row:row + 128, t0 - WOFF:t1 - WOFF], R_GA[seg]
                                elif kind == "UB":
                                    dst, rdst = UB[seg][row:row + 128, t0 - WOFF:t1 - WOFF], R_UB[seg]
                                else:
                                    dst, rdst = GB[seg][row:row + 128, t0 - WOFF:t1 - WOFF], R_GB[seg]
                                em.dma("sp", [(dst, sb[:, 0:ntok])], dsb, reads=[rsb], writes=[rdst], accum=True)
                        if nxt_tiles and li >= 2:
                            build_tile(p + 1, nxt_tiles.pop(0))
                    while nxt_tiles:
                        build_tile(p + 1, nxt_tiles.pop(0))

        if "B" in phases:
            with phase() as ph:
                QM = SEGS[0].nQ
                WM = SEGS[0].W
                NTN = SEGS[0].nqt + 4
                WS = WM // 16
                QS = QM // 16
                kth = Ring(em, ph, nc, "kth", 2, [128, WM], BF16, dsem=True)
                vh = Ring(em, ph, nc, "vh", 2, [128, NTN, 128], BF16, dsem=True)
                qth = Ring(em, ph, nc, "qth", 2, [128, QM], BF16, dsem=True)
                gah = Ring(em, ph, nc, "gah", 1, [128, QM], BF16, dsem=True)
                KR = Ring(em, ph, nc, "KR", 1, [128, 16, WS], BF16)
                QR = Ring(em, ph, nc, "QR", 1, [128, 16, QS], BF16)
                v3r = Ring(em, ph, nc, "v3r", 1, [128, 16, 3, 128], BF16, dsem=True)
                hk12 = Ring(em, ph, nc, "hk12", 2, [128, 5, 128], BF16, dsem=True)
                eb12 = Ring(em, ph, nc, "eb12", 2, [128, 5, 128], BF16)
                hk3 = Ring(em, ph, nc, "hk3", 2, [128, 2, 128], BF16, dsem=True)
                eb3 = Ring(em, ph, nc, "eb3", 2, [128, 2, 128], BF16)
                ogst = Ring(em, ph, nc, "ogst", 1, [128, QM], BF16, dsem=True)
                sgr = Ring(em, ph, nc, "sgr", 2, [128, QM], BF16)
                pr = Ring(em, ph, nc, "pr", 6, [128, 512], BF16)
                accN = ph.enter_context(nc.sbuf_tensor("accN", [128, QM], F32))
                accD = ph.enter_context(nc.sbuf_tensor("accD", [128, QM], F32))
                tmpf = ph.enter_context(nc.sbuf_tensor("tmpf", [128, QM], F32))
                R_accN, R_accD, R_tmpf = Res(), Res(), Res()
                sbank = Ring(em, ph, nc, "sbank", 4, [128, 512], F32, psum=True)
                numb = Ring(em, ph, nc, "numb", 2, [128, 512], F32, psum=True)
                denb = Ring(em, ph, nc, "denb", 2, [128, 512], F32, psum=True)
                pob = sbank
                onesv = ph.enter_context(nc.sbuf_tensor("onesv", [128, NTN, 128], BF16))
                vldt = ph.enter_context(nc.sbuf_tensor("vldt", [128, NTN], F32))
                ones3 = ph.enter_context(nc.sbuf_tensor("ones3", [128, 16, 3, 128], BF16))
                vld3 = ph.enter_context(nc.sbuf_tensor("vld3", [128, 16, 3], F32))
                R_onesv, R_vldt, R_ones3, R_vld3 = Res(), Res(), Res(), Res()
                d_b = em.dsem()

                ub = Ring(em, ph, nc, "ub", 1, [128, QM], F32, dsem=True)
                gbr = Ring(em, ph, nc, "gbr", 1, [128, QM], BF16, dsem=True)
                sA = ph.enter_context(nc.sbuf_tensor("sA", [128, QM], F32))
                sB = ph.enter_context(nc.sbuf_tensor("sB", [128, QM], F32))
                R_sA, R_sB = Res(), Res()
                plr = Ring(em, ph, nc, "plr", 2, [128, QM], BF16)
                icb = ph.enter_context(nc.sbuf_tensor("icb", [128, QM], F32))
                R_icb = Res()
                pw = ph.enter_context(nc.sbuf_tensor("pw", [128, 4, 2, 256], BF16))
                R_pw = Res()
                psc = ph.enter_context(nc.sbuf_tensor("psc", [128, 8], F32))
                R_psc = Res()
                sgb2 = Ring(em, ph, nc, "sgb2", 1, [128, QM], BF16)
                obt = Ring(em, ph, nc, "obt", 2, [128, 512], F32)
                ogst2 = Ring(em, ph, nc, "ogst2", 1, [128, QM], BF16, dsem=True)
                d_p = em.dsem()
                em.dma("pool", [(pw[:], pool_w.rearrange("g (ic p) o -> p g ic o", p=128))], em.dsem(), writes=[R_pw])
                em.dma("sp", [(psc[:], pscale_d)], em.dsem(), writes=[R_psc])
                for buf, rr in ((sA, R_sA), (sB, R_sB)):
                    em.op("pool", lambda e, buf=buf: e.memset(buf[:], 0.0), writes=[rr])
                for i in range(2):
                    b_, r_, _ = plr.next()
                    em.op("pool", lambda e, b_=b_: e.memset(b_[:], 0.0), writes=[r_])

                def pool_items(seg):
                    sg = SEGS[seg]
                    n = sg.nQ
                    for g in range(4):
                        w = POOL_WINDOWS[g]
                        em.dma("sp", [(icb[:, 0:n], bass.AP(icnt[seg].tensor, g * n, [[0, 128], [1, n]]))], d_p, writes=[R_icb])
                        yield
                        pls = []
                        for ic in range(2):
                            u, ru, du = ub.next()
                            r0 = 256 * g + 128 * ic
                            em.dma("sp", [(u[:, 0:n], UB[seg][r0:r0 + 128, :])], du, reads=[R_UB[seg]], writes=[ru])
                            yield
                            em.op("dve", lambda e, u=u: e.tensor_tensor(sA[:, 1:n], u[:, 0:n - 1], u[:, 1:n], ALU.add), reads=[ru], writes=[R_sA])
                            yield
                            cur, rcur, oth, roth = sA, R_sA, sB, R_sB
                            lo, hi = 1, n
                            step = 1
                            ww = 2
                            while ww < w:
                                nlo, nhi = lo + step, hi - step
                                em.op("dve", lambda e, cur=cur, oth=oth, nlo=nlo, nhi=nhi, step=step: e.tensor_tensor(
                                    oth[:, nlo:nhi], cur[:, nlo - step:nhi - step], cur[:, nlo + step:nhi + step], ALU.add),
                                    reads=[rcur], writes=[roth])
                                yield
                                cur, rcur, oth, roth = oth, roth, cur, rcur
                                lo, hi = nlo, nhi
                                step *= 2
                                ww *= 2
                            em.op("dve", lambda e, cur=cur, oth=oth: e.tensor_tensor(oth[:, 8:n - 8], cur[:, 8:n - 8], icb[:, 8:n - 8], ALU.mult),
                                  reads=[rcur, R_icb], writes=[roth])
                            yield
                            pl, rpl, _ = plr.next()
                            em.op("dve", lambda e, oth=oth, u=u, pl=pl: e.tensor_tensor(pl[:, 8:n - 8], oth[:, 8:n - 8], u[:, 8:n - 8], ALU.subtract),
                                  reads=[roth, ru], writes=[rpl])
                            yield
                            pls.append((pl, rpl))
                            yield
                        for oc in range(2):
                            r0 = 256 * g + 128 * oc
                            gbb, rgb, dgb = gbr.next()
                            em.dma("sp", [(gbb[:, 0:n], GB[seg][r0:r0 + 128, :])], dgb, reads=[R_GB[seg]], writes=[rgb])
                            yield
                            s2, rs2, _ = sgb2.next()
                            em.op("act", lambda e, s2=s2, gbb=gbb: e.activation(s2[:, 0:n], gbb[:, 0:n], AF.Silu), reads=[rgb], writes=[rs2])
                            yield
                            og2, rog2, dog2 = ogst2.next()
                            for (off, nn) in split_blocks(n):
                                bank, rbank, _ = pob.next()
                                em.group("pe", [(lambda e, ic=ic, bank=bank, off=off, nn=nn: e.matmul(bank[:, 0:nn], pw[:, g, ic, 128 * oc:128 * oc + 128], pls[ic][0][:, off:off + nn],
                                                                                         start=(ic == 0), stop=(ic == 1))) for ic in range(2)],
                                         reads=[pls[0][1], pls[1][1], R_pw], writes=[rbank])
                                ob, rob, _ = obt.next()
                                em.op("act", lambda e, ob=ob, bank=bank, nn=nn, ch=2 * g + oc: e.activation(ob[:, 0:nn], bank[:, 0:nn], AF.Identity, scale=psc[:, ch:ch + 1]),
                                      reads=[rbank, R_psc], writes=[rob])
                                yield
                                em.op("dve", lambda e, ob=ob, off=off, nn=nn, og2=og2, s2=s2: e.tensor_tensor(og2[:, off:off + nn], ob[:, 0:nn], s2[:, off:off + nn], ALU.mult),
                                      reads=[rob, rs2], writes=[rog2])
                                yield
                            em.dma("sp", [(OG[seg][2048 + r0:2048 + r0 + 128, :], og2[:, 0:n])], dog2, reads=[rog2], writes=[R_OG[seg]], accum=True)
                            yield

                def sb_ap(t, off, dims):
                    return bass.AP(t, off, [list(t[:].ap[0])] + [list(d_) for d_ in dims])

                for seg in range(2):
                    sg = SEGS[seg]
                    ntn = sg.nqt + 4
                    ws = sg.W // 16
                    qs_n = sg.nQ // 16
                    pitems = pool_items(seg)
                    pcount = [0]
                    prate = max(1, (sg.nqt * 2 * NH) // 230)
                    k3t = [(k3, 128 * k3, min(128, ws - 128 * k3)) for k3 in range((ws + 127) // 128)]
                    em.dma("sp", [(vldt[:, 0:ntn], vld[seg][768:768 + 128 * ntn].rearrange("(t j) -> j t", j=128))], d_b, writes=[R_vldt],
                           allow_slow_non_contiguous=True)
                    em.op("dve", lambda e, ntn=ntn: e.tensor_copy(onesv[:, 0:ntn, :], vldt[:, 0:ntn].unsqueeze(2).to_broadcast([128, ntn, 128])),
                          reads=[R_vldt], writes=[R_onesv])
                    em.op("pool", lambda e: e.memset(vld3[:], 0.0), writes=[R_vld3])
                    em.dma("sp", [(vld3[0:rows, :, k3], bass.AP(vld[seg].tensor, 16 * st_, [[16, rows], [1, 16]])) for (k3, st_, rows) in k3t], d_b,
                           writes=[R_vld3], allow_slow_non_contiguous=True)
                    em.op("dve", lambda e: e.tensor_copy(ones3[:], vld3[:].unsqueeze(3).to_broadcast([128, 16, 3, 128])),
                          reads=[R_vld3], writes=[R_ones3])

                    def load_main(h, seg=seg, sg=sg, ntn=ntn):
                        kb, rk, dk_ = kth.next()
                        em.dma("sp", [(kb[:, 0:sg.W], KT[seg][128 * h:128 * h + 128, :])], dk_, reads=[R_KT[seg]], writes=[rk])
                        qb, rq, dq = qth.next()
                        em.dma("sp", [(qb[:, 0:sg.nQ], QT[seg][128 * h:128 * h + 128, :])], dq, reads=[R_QT[seg]], writes=[rq])
                        h12, rh12, dh12 = hk12.next()
                        em.dma("sp", [(h12[:], bass.AP(VECR, h * NV, [[1, 128], [128, 5], [1, 128]]))], dh12, reads=[R_VECR], writes=[rh12])
                        h3, rh3, dh3 = hk3.next()
                        em.dma("sp", [(h3[:], bass.AP(VECR, h * NV + NV12, [[1, 128], [128, 2], [1, 128]]))], dh3, reads=[R_VECR], writes=[rh3])
                        vb, rv, dv = vh.next()
                        em.dma("sp", [(vb[:, 0:ntn, :], VS[seg][768:768 + 128 * ntn, 128 * h:128 * h + 128].rearrange("(t j) d -> j t d", j=128))], dv,
                               reads=[R_VS[seg]], writes=[rv])
                        gb_, rg, dg = gah.next()
                        em.dma("sp", [(gb_[:, 0:sg.nQ], GA[seg][128 * h:128 * h + 128, :])], dg, reads=[R_GA[seg]], writes=[rg])
                        return (kb, rk, vb, rv, qb, rq, gb_, rg, h12, rh12, h3, rh3)

                    def load_v3(h, seg=seg, k3t=k3t):
                        v3, rv3, dv3 = v3r.next()
                        em.dma("sp", [(v3[0:rows, :, k3, :], bass.AP(VS[seg].tensor, (16 * st_) * D + 128 * h, [[16 * D, rows], [D, 16], [1, 128]]))
                                      for (k3, st_, rows) in k3t], dv3, reads=[R_VS[seg]], writes=[rv3])
                        return (v3, rv3)

                    def prep_steps(hd):
                        kb, rk, vb, rv, qb, rq, gb_, rg, h12, rh12, h3, rh3 = hd
                        e12, re12, _ = eb12.next()
                        e3, re3, _ = eb3.next()
                        kr_, rkr, _ = KR.next()
                        qr_, rqr, _ = QR.next()
                        sgb, rsg, _ = sgr.next()
                        steps = []
                        steps.append(lambda: em.op("act", lambda e: e.activation(e12[:], sb_ap(h12, 127, [[128, 5], [-1, 128]]), AF.Copy), reads=[rh12], writes=[re12]))
                        steps.append(lambda: em.op("act", lambda e: e.activation(e3[:], sb_ap(h3, 127, [[128, 2], [-1, 128]]), AF.Copy), reads=[rh3], writes=[re3]))
                        for g4 in range(4):
                            steps.append(lambda g4=g4: em.op("dve", lambda e: e.tensor_copy(kr_[:, 4 * g4:4 * g4 + 4, 0:ws], sb_ap(kb, 4 * g4, [[1, 4], [16, ws]])),
                                                             reads=[rk], writes=[rkr]))
                        for g8 in range(2):
                            steps.append(lambda g8=g8: em.op("dve", lambda e: e.tensor_copy(qr_[:, 8 * g8:8 * g8 + 8, 0:qs_n], sb_ap(qb, 8 * g8, [[1, 8], [16, qs_n]])),
                                                             reads=[rq], writes=[rqr]))
                        steps.append(lambda: em.op("act", lambda e: e.activation(sgb[:, 0:sg.nQ], gb_[:, 0:sg.nQ], AF.Silu), reads=[rg], writes=[rsg]))
                        return steps, (e12, re12, e3, re3, kr_, rkr, qr_, rqr, sgb, rsg)

                    hd_cur = load_main(0)
                    v3_cur = load_v3(0)
                    st0, pp_cur = prep_steps(hd_cur)
                    for st_fn in st0:
                        st_fn()
                    for h in range(NH):
                        kb, rk, vb, rv, qb, rq, gb_, rg, h12, rh12, h3, rh3 = hd_cur
                        v3, rv3 = v3_cur
                        e12, re12, e3, re3, kr_, rkr, qr_, rqr, sgb, rsg = pp_cur
                        nsteps = []
                        if h + 1 < NH:
                            hd_nxt = load_main(h + 1)
                        nxt_box = {}
                        ogb, rog, dog = ogst.next()

                        units = []
                        for (q0_, nq) in [(o_, min(128, qs_n - o_)) for o_ in range(0, qs_n, 128)]:
                            ng = max(1, min(16, 256 // nq))
                            for g0 in range(0, 16, ng):
                                units.append((q0_, nq, g0, min(ng, 16 - g0)))
                        items = [("r", u_i, q0_, nq, g0, ng) for u_i, (q0_, nq, g0, ng) in enumerate(units)]
                        items.append(("prefetch",))
                        for m in range(sg.nqt):
                            items.append(("n", m, 0, 3))
                            items.append(("n", m, 3, 2))
                        nd_state = {}

                        def emit_qk(it):
                            sb_, rsb_, _ = sbank.next()
                            pb, rpb, _ = pr.next()
                            if it[0] == "n":
                                _, m, k0, nk = it
                                em.group("pe", [(lambda e, kk=kk: e.matmul(sb_[:, 128 * kk:128 * kk + 128], kb[:, 128 * (m + 6 + k0 + kk):128 * (m + 6 + k0 + kk) + 128],
                                                                        qb[:, 128 * m:128 * m + 128], start=True, stop=True))
                                                for kk in range(nk)], reads=[rk, rq], writes=[rsb_])
                                em.op("act", lambda e: e.activation(pb[:, 0:128 * nk], sb_[:, 0:128 * nk], AF.Exp, scale=SCALE), reads=[rsb_], writes=[rpb])
                                em.op("dve", lambda e: e.tensor_tensor(pb[:, 0:128 * nk], pb[:, 0:128 * nk], e12[:, k0:k0 + nk, :].rearrange("p a b -> p (a b)"), ALU.mult),
                                      reads=[rpb, re12], writes=[rpb])
                            else:
                                _, u_i, q0_, nq, g0, ng = it
                                tiles = [(k3, st_, rows) for (k3, st_, rows) in k3t if st_ in (q0_, q0_ + 128)]
                                fns = []
                                for gi in range(ng):
                                    g = g0 + gi
                                    for ti, (k3, st_, rows) in enumerate(tiles):
                                        c0 = (ti * ng + gi) * nq
                                        fns.append(lambda e, g=g, st_=st_, rows=rows, c0=c0: e.matmul(sb_[0:rows, c0:c0 + nq], kr_[:, g, st_:st_ + rows],
                                                                                                  qr_[:, g, q0_:q0_ + nq], start=True, stop=True))
                                em.group("pe", fns, reads=[rkr, rqr], writes=[rsb_])
                                if len(tiles) == 2 and tiles[0][2] == 128 and tiles[1][2] == 128:
                                    em.op("act", lambda e: e.activation(pb[:, 0:2 * ng * nq], sb_[:, 0:2 * ng * nq], AF.Exp, scale=SCALE), reads=[rsb_], writes=[rpb])
                                    em.op("dve", lambda e: e.tensor_tensor(sb_ap(pb, 0, [[ng * nq, 2], [nq, ng], [1, nq]]), sb_ap(pb, 0, [[ng * nq, 2], [nq, ng], [1, nq]]),
                                                                           sb_ap(e3, 0, [[128, 2], [0, ng], [1, nq]]), ALU.mult), reads=[rpb, re3], writes=[rpb])
                                    tiles_e = []
                                else:
                                    tiles_e = tiles
                                for ti, (k3, st_, rows) in enumerate(tiles_e):
                                    em.op("act", lambda e, ti=ti, rows=rows: e.activation(pb[0:rows, ti * ng * nq:(ti + 1) * ng * nq], sb_[0:rows, ti * ng * nq:(ti + 1) * ng * nq],
                                                                                        AF.Exp, scale=SCALE), reads=[rsb_], writes=[rpb])
                                    em.op("dve", lambda e, ti=ti, rows=rows: e.tensor_tensor(sb_ap(pb, ti * ng * nq, [[nq, ng], [1, nq]])[0:rows], sb_ap(pb, ti * ng * nq, [[nq, ng], [1, nq]])[0:rows],
                                                                                           sb_ap(e3, 128 * ti, [[0, ng], [1, nq]])[0:rows], ALU.mult),
                                          reads=[rpb, re3], writes=[rpb])
                            return (it, pb, rpb)

                        def emit_pv(it, pb, rpb):
                            if it[0] == "n":
                                _, m, k0, nk = it
                                slot = m % 4
                                if k0 == 0:
                                    if slot == 0:
                                        nd_state["n"] = (numb.next(), denb.next())
                                (nu, rnu, _), (de_, rde, _) = nd_state["n"]
                                fns = []
                                for kk in range(nk):
                                    kt = k0 + kk
                                    fns.append(lambda e, kk=kk, kt=kt: e.matmul(nu[:, 128 * slot:128 * slot + 128], vb[:, m + kt, :], pb[:, 128 * kk:128 * kk + 128],
                                                                           start=(kt == 0), stop=(kt == 4)))
                                for kk in range(nk):
                                    kt = k0 + kk
                                    fns.append(lambda e, kk=kk, kt=kt: e.matmul(de_[:, 128 * slot:128 * slot + 128], onesv[:, m + kt, :], pb[:, 128 * kk:128 * kk + 128],
                                                                           start=(kt == 0), stop=(kt == 4)))
                                em.group("pe", fns, reads=[rv, rpb, R_onesv], writes=[rnu, rde])
                                if k0 == 3 and (slot == 3 or m == sg.nqt - 1):
                                    m0 = m - slot
                                    wdt = 128 * (slot + 1)
                                    em.op("dve", lambda e: e.tensor_tensor(accN[:, 128 * m0:128 * m0 + wdt], accN[:, 128 * m0:128 * m0 + wdt], nu[:, 0:wdt], ALU.add),
                                          reads=[rnu, R_accN], writes=[R_accN])
                                    em.op("dve", lambda e: e.tensor_tensor(accD[:, 128 * m0:128 * m0 + wdt], accD[:, 128 * m0:128 * m0 + wdt], de_[:, 0:wdt], ALU.add),
                                          reads=[rde, R_accD], writes=[R_accD])
                                    c0_, c1_ = 128 * m0, 128 * m0 + wdt
                                    em.op("act", lambda e: e.activation(tmpf[:, c0_:c1_], accD[:, c0_:c1_], AF.Ln), reads=[R_accD], writes=[R_tmpf])
                                    em.op("act", lambda e: e.activation(tmpf[:, c0_:c1_], tmpf[:, c0_:c1_], AF.Exp, scale=-1.0), reads=[R_tmpf], writes=[R_tmpf])
                                    em.op("dve", lambda e: e.tensor_tensor(tmpf[:, c0_:c1_], accN[:, c0_:c1_], tmpf[:, c0_:c1_], ALU.mult), reads=[R_accN, R_tmpf], writes=[R_tmpf])
                                    em.op("dve", lambda e: e.tensor_tensor(ogb[:, c0_:c1_], tmpf[:, c0_:c1_], sgb[:, c0_:c1_], ALU.mult), reads=[R_tmpf, rsg], writes=[rog])
                            else:
                                _, u_i, q0_, nq, g0, ng = it
                                tiles = [(k3, st_, rows) for (k3, st_, rows) in k3t if st_ in (q0_, q0_ + 128)]
                                (nu, rnu, _), (de_, rde, _) = (numb.next(), denb.next())
                                fns = []
                                for gi in range(ng):
                                    g = g0 + gi
                                    col = gi * nq
                                    for ti, (k3, st_, rows) in enumerate(tiles):
                                        c0 = (ti * ng + gi) * nq
                                        fns.append(lambda e, g=g, k3=k3, rows=rows, c0=c0, col=col, ti=ti: e.matmul(nu[:, col:col + nq], v3[0:rows, g, k3, :], pb[0:rows, c0:c0 + nq],
                                                                                                                start=(ti == 0), stop=(ti == len(tiles) - 1)))
                                    for ti, (k3, st_, rows) in enumerate(tiles):
                                        c0 = (ti * ng + gi) * nq
                                        fns.append(lambda e, g=g, k3=k3, rows=rows, c0=c0, col=col, ti=ti: e.matmul(de_[:, col:col + nq], ones3[0:rows, g, k3, :], pb[0:rows, c0:c0 + nq],
                                                                                                                start=(ti == 0), stop=(ti == len(tiles) - 1)))
                                em.group("pe", fns, reads=[rv3, rpb, R_ones3], writes=[rnu, rde])
                                dstn = sb_ap(accN, 16 * q0_ + g0, [[16, nq], [1, ng]])
                                em.op("act", lambda e: e.activation(dstn, sb_ap(nu, 0, [[1, nq], [nq, ng]]), AF.Copy), reads=[rnu], writes=[R_accN])
                                dstd = sb_ap(accD, 16 * q0_ + g0, [[16, nq], [1, ng]])
                                em.op("dve", lambda e: e.tensor_copy(dstd, sb_ap(de_, 0, [[1, nq], [nq, ng]])), reads=[rde], writes=[R_accD])

                        LAG = 4
                        pend = []
                        for it in items:
                            if it[0] == "prefetch":
                                while pend:
                                    emit_pv(*pend.pop(0))
                                if h + 1 < NH:
                                    v3_nxt = load_v3(h + 1)
                                    nsteps, pp_nxt = prep_steps(hd_nxt)
                                continue
                            pend.append(emit_qk(it))
                            if len(pend) > LAG:
                                emit_pv(*pend.pop(0))
                            if it[0] == "n" and it[2] == 3 and nsteps:
                                nsteps.pop(0)()
                            if it[0] == "n":
                                pcount[0] += 1
                                if pcount[0] % prate == 0:
                                    next(pitems, None)
                        while pend:
                            emit_pv(*pend.pop(0))
                        n_ = sg.nQ
                        em.dma("sp", [(OG[seg][128 * h:128 * h + 128, :], ogb[:, 0:n_])], dog, reads=[rog], writes=[R_OG[seg]], accum=True)
                        while nsteps:
                            nsteps.pop(0)()
                        if h + 1 < NH:
                            hd_cur, v3_cur, pp_cur = hd_nxt, v3_nxt, pp_nxt
                    for _ in pitems:
                        pass

        def ln_epilogue(lb, M, banks, rbanks, xres, rxres, out_t, rout):
            ztr, junkr, statr, gbt, rgb_, bbt, rbb = lb
            zt, rzt, _ = ztr.next()
            junk, rjunk, _ = junkr.next()
            stat, rstat, _ = statr.next()
            for s4 in range(4):
                em.op("dve", lambda e, s4=s4: e.scalar_tensor_tensor(zt[0:M, 512 * s4:512 * s4 + 512], xres[0:M, 512 * s4:512 * s4 + 512], ALPHA,
                                                                   banks[s4][0:M, :], ALU.mult, ALU.add),
                      reads=[rxres, rbanks[s4]], writes=[rzt])
            em.op("dve", lambda e: e.reduce_sum(stat[0:M, 0:1], zt[0:M, :], AX.X), reads=[rzt], writes=[rstat])
            em.op("dve", lambda e: e.tensor_scalar(stat[0:M, 1:2], stat[0:M, 0:1], -1.0 / D, None, ALU.mult), reads=[rstat], writes=[rstat])
            em.op("act", lambda e: e.activation(junk[0:M, :], zt[0:M, :], AF.Square, bias=stat[0:M, 1:2], scale=1.0, accum_out=stat[0:M, 2:3]),
                  reads=[rzt, rstat], writes=[rjunk, rstat])
            em.op("dve", lambda e: e.tensor_scalar(stat[0:M, 3:4], stat[0:M, 2:3], 1.0 / D, LN_EPS, ALU.mult, ALU.add), reads=[rstat], writes=[rstat])
            em.op("act", lambda e: e.activation(stat[0:M, 6:7], stat[0:M, 3:4], AF.Sqrt), reads=[rstat], writes=[rstat])
            em.op("dve", lambda e: e.reciprocal(stat[0:M, 4:5], stat[0:M, 6:7]), reads=[rstat], writes=[rstat])
            em.op("dve", lambda e: e.tensor_tensor(stat[0:M, 5:6], stat[0:M, 1:2], stat[0:M, 4:5], ALU.mult), reads=[rstat], writes=[rstat])
            em.op("act", lambda e: e.activation(out_t[0:M, :], zt[0:M, :], AF.Identity, bias=stat[0:M, 5:6], scale=stat[0:M, 4:5]),
                  reads=[rzt, rstat], writes=[rout])
            em.op("dve", lambda e: e.tensor_tensor(out_t[0:M, :], out_t[0:M, :], gbt[0:M, :], ALU.mult), reads=[rout, rgb_], writes=[rout])
            em.op("pool", lambda e: e.tensor_tensor(out_t[0:M, :], out_t[0:M, :], bbt[0:M, :], ALU.add), reads=[rout, rbb], writes=[rout])

        def ln_bufs(ph, layer):
            ztr = Ring(em, ph, nc, "zt%d_" % layer, 2, [128, D], F32)
            junkr = Ring(em, ph, nc, "junk%d_" % layer, 2, [128, D], BF16)
            statr = Ring(em, ph, nc, "stat%d_" % layer, 2, [128, 8], F32)
            gbt = ph.enter_context(nc.sbuf_tensor("gbt%d" % layer, [128, D], F32))
            bbt = ph.enter_context(nc.sbuf_tensor("bbt%d" % layer, [128, D], F32))
            rgb_, rbb = Res(), Res()
            em.dma("sp", [(gbt[:], bass.AP(ln_g.tensor, layer * D, [[0, 128], [1, D]]))], em.dsem(), writes=[rgb_])
            em.dma("sp", [(bbt[:], bass.AP(ln_b.tensor, layer * D, [[0, 128], [1, D]]))], em.dsem(), writes=[rbb])
            return (ztr, junkr, statr, gbt, rgb_, bbt, rbb)

        if "C" in phases:
            with phase() as ph:
                wo = ph.enter_context(nc.sbuf_tensor("wo", [128, 24, D], BF16))
                R_wos = [Res() for _ in range(4)]
                for c in range(4):
                    em.dma("pool", [(wo[:, :, 512 * c:512 * c + 512], w_out_ab[:, 512 * c:512 * c + 512].rearrange("(k p) n -> p k n", p=128))], em.dsem(), writes=[R_wos[c]])
                lb = ln_bufs(ph, 0)
                ogt = Ring(em, ph, nc, "ogt", 2, [128, 24, 128], BF16, dsem=True)
                xres = Ring(em, ph, nc, "xres", 2, [128, D], F32, dsem=True)
                x1t = Ring(em, ph, nc, "x1t", 2, [128, D], F32, dsem=True)
                x1b = Ring(em, ph, nc, "x1b", 2, [128, D], BF16)
                x1Ts = Ring(em, ph, nc, "x1Ts", 2, [128, 16, 128], BF16, dsem=True)
                banks = [ph.enter_context(nc.psum_tensor("cb%d" % i, [128, 512], F32)) for i in range(4)]
                rbanks = [Res() for _ in range(4)]
                tr = Ring(em, ph, nc, "trC", 2, [128, 8, 128], BF16, psum=True)
                tiles = []
                for seg in range(2):
                    tiles += [(seg, "own", i) for i in range(SEGS[seg].nOwn // 128)] + [(seg, "edge", 0)]

                def c_load(t):
                    seg, kind, i = t
                    sg = SEGS[seg]
                    ob, rob, dob = ogt.next()
                    xr, rxr, dxr = xres.next()
                    if kind == "own":
                        q0 = E + 128 * i
                        em.dma("sp", [(ob[:, :, :], OG[seg][:, q0:q0 + 128].rearrange("(c p) q -> p c q", p=128))], dob, reads=[R_OG[seg]], writes=[rob])
                        em.dma("sp", [(xr[:, :], xw[seg][WOFF + q0:WOFF + q0 + 128, :])], dxr, writes=[rxr])
                    else:
                        qa, qb_ = E - 1, E + sg.nOwn
                        em.dma("sp", [(ob[:, :, 0:1], OG[seg][:, qa:qa + 1].rearrange("(c p) q -> p c q", p=128)),
                                      (ob[:, :, 1:2], OG[seg][:, qb_:qb_ + 1].rearrange("(c p) q -> p c q", p=128))], dob,
                               reads=[R_OG[seg]], writes=[rob], allow_slow_non_contiguous=True)
                        em.dma("sp", [(xr[0:1, :], xw[seg][WOFF + qa:WOFF + qa + 1, :]), (xr[1:2, :], xw[seg][WOFF + qb_:WOFF + qb_ + 1, :])], dxr, writes=[rxr])
                    return (ob, rob, xr, rxr)

                def c_mm(t, hd):
                    ob, rob, xr, rxr = hd
                    M = 128 if t[1] == "own" else 2
                    for s4 in range(4):
                        em.group("pe", [(lambda e, k=k, s4=s4: e.matmul(banks[s4][0:M, :], ob[:, k, 0:M], wo[:, k, 512 * s4:512 * s4 + 512],
                                                                        start=(k == 0), stop=(k == 23))) for k in range(24)],
                                 reads=[rob, R_wos[s4]], writes=[rbanks[s4]])

                def c_epi(t, hd):
                    seg, kind, i = t
                    sg = SEGS[seg]
                    ob, rob, xr, rxr = hd
                    M = 128 if kind == "own" else 2
                    xo, rxo, dxo = x1t.next()
                    ln_epilogue(lb, M, banks, rbanks, xr, rxr, xo, rxo)
                    if kind == "own":
                        em.dma("pool", [(X1[seg][1 + 128 * i:1 + 128 * i + 128, :], xo[:, :])], dxo, reads=[rxo], writes=[R_X1[seg]], accum=True)
                    else:
                        em.dma("pool", [(X1[seg][0:1, :], xo[0:1, :]), (X1[seg][sg.nOwn + 1:sg.nOwn + 2, :], xo[1:2, :])], dxo, reads=[rxo], writes=[R_X1[seg]], accum=True)
                    xbb, rxbb, _ = x1b.next()
                    em.op("act", lambda e: e.activation(xbb[0:M, :], xo[0:M, :], AF.Copy), reads=[rxo], writes=[rxbb])
                    return (xbb, rxbb)

                def c_tr(t, cast):
                    seg, kind, i = t
                    sg = SEGS[seg]
                    xbb, rxbb = cast
                    M = 128 if kind == "own" else 2
                    xs, rxs, dxs = x1Ts.next()
                    for g in range(2):
                        pt, rpt, _ = tr.next()
                        em.group("pe", [(lambda e, c=c, pt=pt: e.transpose(pt[:, c % 8, 0:M], xbb[0:M, c * 128:(c + 1) * 128], ident[0:M, 0:M]))
                                        for c in range(8 * g, 8 * g + 8)], reads=[rxbb, R_ident], writes=[rpt])
                        em.op("dve", lambda e, g=g, pt=pt: e.tensor_copy(xs[:, 8 * g:8 * g + 8, 0:M], pt[:, :, 0:M]), reads=[rpt], writes=[rxs])
                    if kind == "own":
                        em.dma("pool", [(X1T[seg][:, 1 + 128 * i:1 + 128 * i + 128].rearrange("(c p) q -> p c q", p=128), xs[:, :, :])], dxs,
                               reads=[rxs], writes=[R_X1T[seg]], accum=True)
                    else:
                        em.dma("pool", [(X1T[seg][:, 0:1].rearrange("(c p) q -> p c q", p=128), xs[:, :, 0:1]),
                                      (X1T[seg][:, sg.nOwn + 1:sg.nOwn + 2].rearrange("(c p) q -> p c q", p=128), xs[:, :, 1:2])], dxs,
                               reads=[rxs], writes=[R_X1T[seg]], allow_slow_non_contiguous=True, accum=True)

                hd_next = c_load(tiles[0])
                prev = None
                for ti, t in enumerate(tiles):
                    hd = hd_next
                    if ti + 1 < len(tiles):
                        hd_next = c_load(tiles[ti + 1])
                    c_mm(t, hd)
                    if prev is not None:
                        c_tr(*prev)
                    cast = c_epi(t, hd)
                    prev = (t, cast)
                c_tr(*prev)

        if "D" in phases:
            with phase() as ph:
                ND0, ND1 = SEGS[0].nD, SEGS[1].nD
                x1T = ph.enter_context(nc.sbuf_tensor("x1Tr", [128, 16, ND0 + ND1], BF16))
                R_x1T = [Res(), Res()]
                d_x = em.dsem(ph)
                segoff = [0, ND0]
                for seg in range(2):
                    n = SEGS[seg].nD
                    d_xs = em.dsem()
                    for c in range(4):
                        em.dma("sp", [(x1T[:, 4 * c:4 * c + 4, segoff[seg]:segoff[seg] + n], X1T[seg][512 * c:512 * c + 512, :].rearrange("(c p) q -> p c q", p=128))],
                               d_xs, reads=[R_X1T[seg]], writes=[R_x1T[seg]], accum=True)
                slab = Ring(em, ph, nc, "slabD", 2, [128, 4, 16, 128], BF16, dsem=True)
                cw = ph.enter_context(nc.sbuf_tensor("cw", [128, 16, 3], F32))
                fl = ph.enter_context(nc.sbuf_tensor("fl", [128, 4], F32))
                R_cw, R_fl = Res(), Res()
                em.dma("sp", [(cw[:], convw_d), (fl[:], flags)], d_x, writes=[R_cw, R_fl])
                cful = Ring(em, ph, nc, "cful", 2, [128, ND0], F32)
                tful = Ring(em, ph, nc, "tful", 2, [128, ND0], F32)
                accr = Ring(em, ph, nc, "accr", 2, [128, SEGS[0].nOwn], F32)
                yst = Ring(em, ph, nc, "yst", 2, [128, SEGS[0].nOwn], BF16, dsem=True)
                vsr = Ring(em, ph, nc, "vsr", 2, [128, 512], F32)
                sgt = Ring(em, ph, nc, "sgt", 2, [128, 512], F32)
                pb4 = [Ring(em, ph, nc, "pd%d" % j, 2, [128, 512], F32, psum=True) for j in range(4)]

                def load_slabD(f):
                    b, rb, db = slab.next()
                    em.dma("pool", [(b[:, j, :, :], w_in_c[:, 2048 * j + 128 * f:2048 * j + 128 * f + 128].rearrange("(k p) n -> p k n", p=128)) for j in range(4)],
                           db, writes=[rb])
                    return b, rb

                nxt = load_slabD(0)
                for f in range(16):
                    b, rb = nxt
                    if f + 1 < 16:
                        nxt = load_slabD(f + 1)
                    for seg in range(2):
                        sg = SEGS[seg]
                        n = sg.nD
                        cf, rcf, _ = cful.next()
                        tf, rtf, _ = tful.next()
                        for (off, nn) in split_blocks(n, 512, 2):
                            col = segoff[seg] + off
                            bk = [pb4[j].next() for j in range(4)]
                            for j in range(4):
                                em.group("pe", [(lambda e, k=k, j=j, col=col, nn=nn, bank=bk[j][0]: e.matmul(bank[:, 0:nn], b[:, j, k, :], x1T[:, k, col:col + nn],
                                                                                                start=(k == 0), stop=(k == 15))) for k in range(16)],
                                         reads=[R_x1T[seg], rb], writes=[bk[j][1]])
                            vs_, rvs, _ = vsr.next()
                            em.op("act", lambda e, vs_=vs_, nn=nn, bank=bk[2][0]: e.activation(vs_[:, 0:nn], bank[:, 0:nn], AF.Copy), reads=[bk[2][1]], writes=[rvs])
                            em.op("dve", lambda e, vs_=vs_, nn=nn, off=off, cf=cf, bank=bk[1][0]: e.tensor_tensor(cf[:, off:off + nn], bank[:, 0:nn], vs_[:, 0:nn], ALU.mult),
                                  reads=[bk[1][1], rvs], writes=[rcf])
                            sg_, rsg_, _ = sgt.next()
                            em.op("act", lambda e, sg_=sg_, nn=nn, bank=bk[3][0]: e.activation(sg_[:, 0:nn], bank[:, 0:nn], AF.Silu), reads=[bk[3][1]], writes=[rsg_])
                            em.op("dve", lambda e, sg_=sg_, nn=nn, off=off, tf=tf, bank=bk[0][0]: e.tensor_tensor(tf[:, off:off + nn], bank[:, 0:nn], sg_[:, 0:nn], ALU.mult),
                                  reads=[bk[0][1], rsg_], writes=[rtf])
                        no = sg.nOwn
                        em.op("pool", lambda e, cf=cf, seg=seg: e.tensor_scalar(cf[:, 0:1], cf[:, 0:1], fl[:, 2 * seg:2 * seg + 1], None, ALU.mult),
                              reads=[rcf, R_fl], writes=[rcf])
                        em.op("pool", lambda e, cf=cf, seg=seg, n=n: e.tensor_scalar(cf[:, n - 1:n], cf[:, n - 1:n], fl[:, 2 * seg + 1:2 * seg + 2], None, ALU.mult),
                              reads=[rcf, R_fl], writes=[rcf])
                        ac, rac, _ = accr.next()
                        em.op("pool", lambda e, ac=ac, cf=cf, no=no: e.tensor_scalar(ac[:, 0:no], cf[:, 0:no], cw[:, f, 0:1], None, ALU.mult),
                              reads=[rcf, R_cw], writes=[rac])
                        em.op("dve", lambda e, ac=ac, cf=cf, no=no: e.scalar_tensor_tensor(ac[:, 0:no], cf[:, 1:no + 1], cw[:, f, 1:2], ac[:, 0:no], ALU.mult, ALU.add),
                              reads=[rcf, R_cw, rac], writes=[rac])
                        em.op("dve", lambda e, ac=ac, cf=cf, no=no: e.scalar_tensor_tensor(ac[:, 0:no], cf[:, 2:no + 2], cw[:, f, 2:3], ac[:, 0:no], ALU.mult, ALU.add),
                              reads=[rcf, R_cw, rac], writes=[rac])
                        ys, rys, dys = yst.next()
                        em.op("pool", lambda e, ac=ac, tf=tf, ys=ys, no=no: e.tensor_tensor(ys[:, 0:no], ac[:, 0:no], tf[:, 1:no + 1], ALU.mult),
                              reads=[rac, rtf], writes=[rys])
                        em.dma("sp", [(YT[seg][128 * f:128 * f + 128, :], ys[:, 0:no])], dys, reads=[rys], writes=[R_YT[seg]], accum=True)

        if "E" in phases:
            with phase() as ph:
                wo = ph.enter_context(nc.sbuf_tensor("woc", [128, 16, D], BF16))
                R_wos = [Res() for _ in range(4)]
                for c in range(4):
                    em.dma("pool", [(wo[:, :, 512 * c:512 * c + 512], w_out_c[:, 512 * c:512 * c + 512].rearrange("(k p) n -> p k n", p=128))], em.dsem(), writes=[R_wos[c]])
                lb = ln_bufs(ph, 1)
                ytt = Ring(em, ph, nc, "ytt", 3, [128, 16, 128], BF16, dsem=True)
                xres = Ring(em, ph, nc, "xresE", 3, [128, D], F32, dsem=True)
                outt = Ring(em, ph, nc, "outt", 2, [128, D], F32, dsem=True)
                banks = [ph.enter_context(nc.psum_tensor("eb%d" % i, [128, 512], F32)) for i in range(4)]
                rbanks = [Res() for _ in range(4)]
                tiles = [(seg, i) for seg in range(2) for i in range(SEGS[seg].nOwn // 128)]

                def e_load(t):
                    seg, i = t
                    yb, ryb, dyb = ytt.next()
                    em.dma("sp", [(yb[:, :, :], YT[seg][:, 128 * i:128 * i + 128].rearrange("(c p) q -> p c q", p=128))], dyb, reads=[R_YT[seg]], writes=[ryb])
                    xr, rxr, dxr = xres.next()
                    em.dma("sp", [(xr[:, :], X1[seg][1 + 128 * i:1 + 128 * i + 128, :])], dxr, reads=[R_X1[seg]], writes=[rxr])
                    return (yb, ryb, xr, rxr)

                pre = [e_load(tiles[0]), e_load(tiles[1])]
                for ti, t in enumerate(tiles):
                    seg, i = t
                    yb, ryb, xr, rxr = pre.pop(0)
                    if ti + 2 < len(tiles):
                        pre.append(e_load(tiles[ti + 2]))
                    for s4 in range(4):
                        em.group("pe", [(lambda e, k=k, s4=s4: e.matmul(banks[s4][:, :], yb[:, k, :], wo[:, k, 512 * s4:512 * s4 + 512],
                                                                        start=(k == 0), stop=(k == 15))) for k in range(16)],
                                 reads=[ryb, R_wos[s4]], writes=[rbanks[s4]])
                    ot, rot, dot = outt.next()
                    ln_epilogue(lb, 128, banks, rbanks, xr, rxr, ot, rot)
                    em.dma("pool", [(yout[seg][128 * i:128 * i + 128, :], ot[:, :])], dot, reads=[rot], writes=[R_Y[seg]], accum=True)

        em.finish([R_Y[0], R_Y[1], R_KT[0], R_KT[1], R_VS[0], R_VS[1], R_QT[0], R_QT[1], R_GA[0], R_GA[1], R_UB[0], R_UB[1],
                   R_GB[0], R_GB[1], R_OG[0], R_OG[1], R_X1[0], R_X1[1], R_X1T[0], R_X1T[1], R_YT[0], R_YT[1], R_VECR])
    return nc


def make_in_maps(x_prompt, x_sample, rel_bias, w_in_ab, pool_w, pool_scale, w_out_ab, w_in_c, conv_w, w_out_c, ln_g, ln_b):
    f32 = np.float32
    x_prompt = np.asarray(x_prompt, f32)
    x_sample = np.asarray(x_sample, f32)
    shared = {
        "ident": np.eye(128, dtype=f32),
        "ohv": _onehot_table(),
        "rba": np.concatenate([np.asarray(rel_bias, f32), np.ones((1, NH), f32)], axis=0),
        "w_in_ab": np.ascontiguousarray(np.asarray(w_in_ab, f32)[0]),
        "pool_w": np.ascontiguousarray(np.asarray(pool_w, f32)[0]),
        "pscale": np.ascontiguousarray(np.asarray(pool_scale, f32)[0].reshape(8, 128).T),
        "w_out_ab": np.ascontiguousarray(np.asarray(w_out_ab, f32)[0]),
        "w_in_c": np.ascontiguousarray(np.asarray(w_in_c, f32)[0]),
        "convw": np.ascontiguousarray(np.asarray(conv_w, f32)[0].reshape(3, 16, 128).transpose(2, 1, 0)),
        "w_out_c": np.ascontiguousarray(np.asarray(w_out_c, f32)[0]),
        "ln_g": np.asarray(ln_g, f32),
        "ln_b": np.asarray(ln_b, f32),
    }
    maps = []
    for c in range(NCORES):
        m = dict(shared)
        fl = np.zeros((128, 4), f32)
        for s in range(2):
            sg = SEGS[s]
            if s == 0:
                seq, a, S = x_sample[0], 2048 * c, S_SAMPLE
            else:
                seq, a, S = x_prompt[c // 2], 1024 * (c % 2), S_PROMPT
            pos = a - E - WOFF + np.arange(sg.W)
            ok = (pos >= 0) & (pos < S)
            xwin = np.zeros((sg.W, D), f32)
            xwin[ok] = seq[pos[ok]]
            m["xw%d" % s] = xwin
            m["vld%d" % s] = ok.astype(f32)
            qpos = a - E + np.arange(sg.nQ)
            ic = np.zeros((4, sg.nQ), f32)
            for g, w in enumerate(POOL_WINDOWS):
                lo = np.maximum(qpos - w // 2, 0)
                hi = np.minimum(qpos + w // 2 - 1, S - 1)
                cnt = np.maximum(hi - lo + 1, 1)
                ic[g] = 1.0 / cnt
            m["icnt%d" % s] = ic
            fl[:, 2 * s] = 1.0 if a - 1 >= 0 else 0.0
            fl[:, 2 * s + 1] = 1.0 if a + sg.nOwn < S else 0.0
        m["flags"] = fl
        maps.append(m)
    return maps


_NC_CACHE = {}


def kernel(x_prompt, x_sample, rel_bias, w_in_ab, pool_w, pool_scale, w_out_ab, w_in_c, conv_w, w_out_c, ln_g, ln_b):
    if "nc" not in _NC_CACHE:
        _NC_CACHE["nc"] = build_program()
    nc = _NC_CACHE["nc"]
    maps = make_in_maps(x_prompt, x_sample, rel_bias, w_in_ab, pool_w, pool_scale, w_out_ab, w_in_c, conv_w, w_out_c, ln_g, ln_b)
    res = run_bass_kernel_spmd(nc, maps, core_ids=list(range(NCORES)))
    y_sample = np.zeros((1, S_SAMPLE, D), np.float32)
    y_prompt = np.zeros((4, S_PROMPT, D), np.float32)
    for c in range(NCORES):
        r = res.results[c]
        y_sample[0, 2048 * c:2048 * c + 2048] = r["y0"]
        y_prompt[c // 2, 1024 * (c % 2):1024 * (c % 2) + 1024] = r["y1"]
    return (y_prompt, y_sample)
```

```python
import math
from contextlib import ExitStack, contextmanager

import numpy as np
import concourse.bass as bass
import concourse.mybir as mybir
from concourse.bass_utils import run_bass_kernel_spmd

F32 = mybir.dt.float32
BF16 = mybir.dt.bfloat16
AF = mybir.ActivationFunctionType
ALU = mybir.AluOpType
AX = mybir.AxisListType

D = 2048
NH = 16
HD = 128
NCORES = 8
S_SAMPLE = 16384
S_PROMPT = 2048
ALPHA = 4.0 ** 0.25
LN_EPS = 1e-5
SCALE = HD ** -0.5
POOL_WINDOWS = (2, 4, 8, 16)

E = 128
WOFF = 1024
NKT = 17
NV12 = 768
NV = 1152
NEG = -30000.0


class Seg:
    def __init__(self, n_own):
        self.nOwn = n_own
        self.nQ = n_own + 2 * E
        self.W = self.nQ + 2048
        self.T = self.W // 128
        self.nqt = self.nQ // 128
        self.nD = n_own + 2


SEGS = [Seg(2048), Seg(1024)]
PASSES = [(0, 0, 2176), (0, 2176, 4352), (1, 0, 1664), (1, 1664, 3328)]
XT_MAX = 2176


def split_blocks(n, maxn=512, gran=64):
    k = -(-n // maxn)
    base = -(-n // k)
    base = -(-base // gran) * gran
    out = []
    off = 0
    while off < n:
        m = min(base, n - off)
        out.append((off, m))
        off += m
    return out


class Res:
    __slots__ = ("name", "w", "r")

    def __init__(self, name=""):
        self.name = name
        self.w = {}
        self.r = {}


class _Eng:
    def __init__(self, name, h, sem):
        self.name, self.h, self.sem, self.n, self.waited = name, h, sem, 0, {}


class DSem:
    def __init__(self, sem):
        self.sem, self.n = sem, 0


class Emitter:
    def __init__(self, nc, stack):
        self.nc = nc
        self.E = {}
        for name, h in (("pe", nc.tensor), ("act", nc.scalar), ("dve", nc.vector),
                        ("pool", nc.gpsimd), ("sp", nc.sync)):
            sem = stack.enter_context(nc.semaphore("s_" + name))
            self.E[name] = _Eng(name, h, sem)
        self._stack = stack
        self._nd = 0
        self._all_ds = []

    def dsem(self, stack=None):
        self._nd += 1
        sem = self._stack.enter_context(self.nc.semaphore("d%d" % self._nd))
        d = DSem(sem)
        self._all_ds.append(d)
        return d

    def barrier(self):
        for e in self.E.values():
            for o in self.E.values():
                if o is e or o.n == 0:
                    continue
                k = id(o.sem)
                if e.waited.get(k, 0) < o.n:
                    e.h.wait_ge(o.sem, o.n)
                    e.waited[k] = o.n
            for d in self._all_ds:
                k = id(d.sem)
                if d.n and e.waited.get(k, 0) < d.n:
                    e.h.wait_ge(d.sem, d.n)
                    e.waited[k] = d.n

    def _wait(self, e, reads, writes):
        need = {}
        for r in reads:
            for k, sv in r.w.items():
                if need.get(k, (None, 0))[1] < sv[1]:
                    need[k] = sv
        for w in writes:
            for dd in (w.w, w.r):
                for k, sv in dd.items():
                    if need.get(k, (None, 0))[1] < sv[1]:
                        need[k] = sv
        own = id(e.sem)
        for k, (s, v) in need.items():
            if k == own and e.name == "pe":
                continue
            if e.waited.get(k, 0) < v:
                e.h.wait_ge(s, v)
                e.waited[k] = v

    def _commit(self, ev, reads, writes):
        k = id(ev[0])
        for r in reads:
            if r.r.get(k, (None, 0))[1] < ev[1]:
                r.r[k] = ev
        for w in writes:
            w.w = {k: ev}
            w.r = {}

    def op(self, eng, fn, reads=(), writes=()):
        e = self.E[eng]
        self._wait(e, reads, writes)
        ins = fn(e.h)
        e.n += 1
        ins.then_inc(e.sem, 1)
        self._commit((e.sem, e.n), reads, writes)

    def group(self, eng, fns, reads=(), writes=()):
        e = self.E[eng]
        self._wait(e, reads, writes)
        ins = None
        for fn in fns:
            ins = fn(e.h)
        e.n += 1
        ins.then_inc(e.sem, 1)
        self._commit((e.sem, e.n), reads, writes)

    def dma(self, q, pairs, ds, reads=(), writes=(), accum=False, **kw):
        e = self.E[q]
        if accum:
            saved = [(w, w.w) for w in writes]
            for w in writes:
                w.w = {}
            self._wait(e, reads, writes)
            for w, ww in saved:
                w.w = ww
        else:
            self._wait(e, reads, writes)
        for (o, i) in pairs:
            e.h.dma_start(out=o, in_=i, **kw).then_inc(ds.sem, 16)
            ds.n += 16
        ev = (ds.sem, ds.n)
        if accum:
            k = id(ds.sem)
            for r in reads:
                if r.r.get(k, (None, 0))[1] < ev[1]:
                    r.r[k] = ev
            for w in writes:
                if w.r:
                    w.w = {}
                    w.r = {}
                w.w[k] = ev
        else:
            self._commit(ev, reads, writes)

    def finish(self, resources):
        self._wait(self.E["sp"], resources, ())


class Ring:
    def __init__(self, em, stack, nc, name, n, shape, dtype, psum=False, dsem=False):
        self.bufs, self.res, self.ds = [], [], []
        for i in range(n):
            if psum:
                t = stack.enter_context(nc.psum_tensor("%s%d" % (name, i), list(shape), dtype))
            else:
                t = stack.enter_context(nc.sbuf_tensor("%s%d" % (name, i), list(shape), dtype))
            self.bufs.append(t)
            self.res.append(Res("%s%d" % (name, i)))
            self.ds.append(em.dsem(stack) if dsem else None)
        self.i = 0
        self.n = n

    def next(self):
        j = self.i % self.n
        self.i += 1
        return self.bufs[j], self.res[j], self.ds[j]


def _t5_bucket(rel):
    nb = 16
    max_exact = 8
    ret = np.where(rel > 0, nb, 0)
    n = np.abs(rel)
    n_safe = np.maximum(n, 1).astype(np.float64)
    large = max_exact + (np.log(n_safe / max_exact) / math.log(1024 / max_exact) * (nb - max_exact)).astype(np.int64)
    large = np.minimum(large, nb - 1)
    return (ret + np.where(n < max_exact, n, large)).astype(np.int32)


def _onehot_table():
    oh = np.zeros((33, NV), np.float32)
    n = np.arange(NV12)
    d = n - 383
    ad = np.abs(d)
    mult = (ad <= 64).astype(np.int64) + ((d % 4 == 0) & (ad <= 256))
    oh[_t5_bucket(d), n] = 1.0
    oh[32, :NV12] = np.where(mult > 0, np.log(np.maximum(mult, 1)), NEG).astype(np.float32)
    n3 = np.arange(NV - NV12)
    dl = n3 - 191
    oh[_t5_bucket(16 * dl), NV12 + n3] = 1.0
    oh[32, NV12:] = np.where(np.abs(dl) <= 64, 0.0, NEG).astype(np.float32)
    return oh


def build_program(debug=False, phases="ABCDE"):
    nc = bass.Bass("TRN2", target_bir_lowering=False)
    dk = "ExternalOutput" if debug else "Internal"

    def din(name, shape, dt=F32):
        return nc.dram_tensor(name, list(shape), dt, kind="ExternalInput").ap()

    def dscr(name, shape, dt):
        return nc.dram_tensor(name, list(shape), dt, kind=dk).ap()

    xw = [din("xw%d" % s, [SEGS[s].W, D]) for s in range(2)]
    vld = [din("vld%d" % s, [SEGS[s].W]) for s in range(2)]
    icnt = [din("icnt%d" % s, [4, SEGS[s].nQ]) for s in range(2)]
    flags = din("flags", [128, 4])
    ident_d = din("ident", [128, 128])
    ohv_d = din("ohv", [33, NV])
    rba_d = din("rba", [33, NH])
    w_in_ab = din("w_in_ab", [D, 10240])
    pool_w = din("pool_w", [4, 256, 256])
    pscale_d = din("pscale", [128, 8])
    w_out_ab = din("w_out_ab", [3072, D])
    w_in_c = din("w_in_c", [D, 8192])
    convw_d = din("convw", [128, 16, 3])
    w_out_c = din("w_out_c", [D, D])
    ln_g = din("ln_g", [2, D])
    ln_b = din("ln_b", [2, D])

    yout = [nc.dram_tensor("y%d" % s, [SEGS[s].nOwn, D], F32, kind="ExternalOutput").ap() for s in range(2)]

    KT = [dscr("KT%d" % s, [D, SEGS[s].W], BF16) for s in range(2)]
    VS = [dscr("VS%d" % s, [SEGS[s].W, D], BF16) for s in range(2)]
    QT = [dscr("QT%d" % s, [D, SEGS[s].nQ], BF16) for s in range(2)]
    GA = [dscr("GA%d" % s, [D, SEGS[s].nQ], BF16) for s in range(2)]
    UB = [dscr("UB%d" % s, [1024, SEGS[s].nQ], F32) for s in range(2)]
    GB = [dscr("GB%d" % s, [1024, SEGS[s].nQ], BF16) for s in range(2)]
    OG = [dscr("OG%d" % s, [3072, SEGS[s].nQ], BF16) for s in range(2)]
    X1 = [dscr("X1_%d" % s, [SEGS[s].nD, D], F32) for s in range(2)]
    X1T = [dscr("X1T%d" % s, [D, SEGS[s].nD], BF16) for s in range(2)]
    YT = [dscr("YT%d" % s, [D, SEGS[s].nOwn], BF16) for s in range(2)]
    VECR = nc.dram_tensor("VECR", [NH, NV], BF16, kind=dk)

    R_KT = [Res() for _ in range(2)]
    R_VS = [Res() for _ in range(2)]
    R_QT = [Res() for _ in range(2)]
    R_GA = [Res() for _ in range(2)]
    R_UB = [Res() for _ in range(2)]
    R_GB = [Res() for _ in range(2)]
    R_OG = [Res() for _ in range(2)]
    R_X1 = [Res() for _ in range(2)]
    R_X1T = [Res() for _ in range(2)]
    R_YT = [Res() for _ in range(2)]
    R_VECR = Res()
    R_Y = [Res() for _ in range(2)]

    with ExitStack() as top:
        em = Emitter(nc, top)

        @contextmanager
        def phase():
            with ExitStack() as _ph:
                yield _ph
                em.barrier()

        ident = top.enter_context(nc.sbuf_tensor("identb", [128, 128], BF16))
        R_ident = Res()
        d_const = em.dsem()
        em.dma("pool", [(ident[:], ident_d)], d_const, writes=[R_ident])

        if "A" in phases or "B" in phases:
            with phase() as ph:
                ohv = ph.enter_context(nc.sbuf_tensor("ohvb", [33, NV], F32))
                rba = ph.enter_context(nc.sbuf_tensor("rbab", [33, NH], F32))
                vst = ph.enter_context(nc.sbuf_tensor("vecst", [NH, NV], BF16))
                pv = ph.enter_context(nc.psum_tensor("pvec", [128, 512], F32))
                R_ohv, R_rba, R_vst, R_pv = Res(), Res(), Res(), Res()
                dd = em.dsem(ph)
                em.dma("sp", [(ohv[:], ohv_d), (rba[:], rba_d)], dd, writes=[R_ohv, R_rba])
                for c0 in range(0, NV, 512):
                    n = min(512, NV - c0)
                    em.op("pe", lambda e, c0=c0, n=n: e.matmul(pv[0:NH, 0:n], rba[:, :], ohv[:, c0:c0 + n], start=True, stop=True),
                          reads=[R_ohv, R_rba], writes=[R_pv])
                    em.op("act", lambda e, c0=c0, n=n: e.activation(vst[:, c0:c0 + n], pv[0:NH, 0:n], AF.Exp),
                          reads=[R_pv], writes=[R_vst])
                em.dma("sp", [(VECR.ap(), vst[:])], dd, reads=[R_vst], writes=[R_VECR])
                em.barrier()

        if "A" in phases:
            with phase() as ph:
                xT = [ph.enter_context(nc.sbuf_tensor("xT%d" % i, [128, 16, XT_MAX], BF16)) for i in range(2)]
                R_xT = [[Res() for _ in range(17)] for _ in range(2)]
                xb = Ring(em, ph, nc, "xb", 2, [128, D], BF16, dsem=True)
                slab = Ring(em, ph, nc, "slab", 4, [128, 16, 256], BF16, dsem=True)
                st16 = Ring(em, ph, nc, "st16", 3, [128, XT_MAX], BF16, dsem=True)
                st32 = Ring(em, ph, nc, "st32", 2, [128, 1216], F32, dsem=True)
                stv = Ring(em, ph, nc, "stv", 4, [128, 256], BF16, dsem=True)
                mm = Ring(em, ph, nc, "mmA", 4, [128, 512], F32, psum=True)
                tr = Ring(em, ph, nc, "trA", 2, [128, 8, 128], BF16, psum=True)
                evi = [0]

                def evac(out_ap, in_ap, reads, writes):
                    evi[0] += 1
                    if evi[0] % 2:
                        em.op("act", lambda e: e.activation(out_ap, in_ap, AF.Copy), reads=reads, writes=writes)
                    else:
                        em.op("dve", lambda e: e.tensor_copy(out_ap, in_ap), reads=reads, writes=writes)

                def build_tile(p, t):
                    seg, w0, w1 = PASSES[p]
                    b, rb, db = xb.next()
                    em.dma("pool", [(b[:], xw[seg][w0 + 128 * t: w0 + 128 * t + 128, :])], db, writes=[rb])
                    for g in range(2):
                        pt, rpt, _ = tr.next()
                        em.group("pe", [(lambda e, c=c, pt=pt: e.transpose(pt[:, c % 8, :], b[:, c * 128:(c + 1) * 128], ident[:]))
                                        for c in range(8 * g, 8 * g + 8)], reads=[rb, R_ident], writes=[rpt])
                        evac(xT[p % 2][:, 8 * g:8 * g + 8, 128 * t:128 * t + 128], pt[:], [rpt], [R_xT[p % 2][t]])

                def slab_list(p):
                    seg, w0, w1 = PASSES[p]
                    sg = SEGS[seg]
                    q0 = max(w0, WOFF)
                    q1 = min(w1, WOFF + sg.nQ)
                    L = []
                    for j in range(8):
                        L.append(("K", 2048 + 256 * j, j))
                    for j in range(8):
                        L.append(("V", 4096 + 256 * j, j))
                    if q1 > q0:
                        for j in range(8):
                            L.append(("Q", 256 * j, j))
                        for j in range(8):
                            L.append(("GA", 6144 + 256 * j, j))
                        for j in range(4):
                            L.append(("UB", 8192 + 256 * j, j))
                        for j in range(4):
                            L.append(("GB", 9216 + 256 * j, j))
                    return L

                all_slabs = []
                for p in range(len(PASSES)):
                    for it in slab_list(p):
                        all_slabs.append((p,) + it)
                loaded = {}

                def load_slab(i):
                    if i >= len(all_slabs) or i in loaded:
                        return
                    p, kind, c0, j = all_slabs[i]
                    b, rb, db = slab.next()
                    em.dma("pool", [(b[:], w_in_ab[:, c0:c0 + 256].rearrange("(k p) n -> p k n", p=128))], db, writes=[rb])
                    loaded[i] = (b, rb)

                built = set()

                def ensure_tiles(p, tl):
                    for t in tl:
                        if (p, t) not in built:
                            built.add((p, t))
                            build_tile(p, t)

                load_slab(0)
                load_slab(1)
                si = 0
                for p in range(len(PASSES)):
                    seg, w0, w1 = PASSES[p]
                    sg = SEGS[seg]
                    ntile = (w1 - w0) // 128
                    q0 = max(w0, WOFF)
                    q1 = min(w1, WOFF + sg.nQ)
                    xt = xT[p % 2]
                    rxt = R_xT[p % 2]
                    nxt_tiles = []
                    if p + 1 < len(PASSES):
                        nxt_tiles = list(range((PASSES[p + 1][2] - PASSES[p + 1][1]) // 128))
                    sl = slab_list(p)
                    for li, (kind, c0, j) in enumerate(sl):
                        load_slab(si + 2)
                        b, rb = loaded.pop(si)
                        si += 1
                        if kind == "V":
                            for t in range(ntile):
                                ensure_tiles(p, [t])
                                bank, rbank, _ = mm.next()
                                em.group("pe", [(lambda e, k=k, t=t, bank=bank, b=b: e.matmul(bank[:, 0:256], xt[:, k, 128 * t:128 * t + 128], b[:, k, :],
                                                                                 start=(k == 0), stop=(k == 15))) for k in range(16)],
                                         reads=[rxt[t], rb], writes=[rbank])
                                sv, rsv, dsv = stv.next()
                                evac(sv[:], bank[:, 0:256], [rbank], [rsv])
                                em.dma("sp", [(VS[seg][w0 + 128 * t:w0 + 128 * t + 128, 256 * j:256 * j + 256], sv[:])], dsv,
                                       reads=[rsv], writes=[R_VS[seg]], accum=True)
                        else:
                            if kind == "K":
                                t0, t1 = w0, w1
                            else:
                                t0, t1 = q0, q1
                            ntok = t1 - t0
                            blocks = split_blocks(ntok)
                            for mc in range(2):
                                if kind == "UB":
                                    sb, rsb, dsb = st32.next()
                                else:
                                    sb, rsb, dsb = st16.next()
                                for (off, n) in blocks:
                                    lo = t0 - w0 + off
                                    tl = list(range(lo // 128, (lo + n - 1) // 128 + 1))
                                    ensure_tiles(p, tl)
                                    bank, rbank, _ = mm.next()
                                    em.group("pe", [(lambda e, k=k, bank=bank, b=b, lo=lo, n=n, mc=mc: e.matmul(bank[:, 0:n], b[:, k, 128 * mc:128 * mc + 128], xt[:, k, lo:lo + n],
                                                                                                  start=(k == 0), stop=(k == 15))) for k in range(16)],
                                             reads=[rxt[t] for t in tl] + [rb], writes=[rbank])
                                    evac(sb[:, off:off + n], bank[:, 0:n], [rbank], [rsb])
                                row = 256 * j + 128 * mc
                                if kind == "K":
                                    dst, rdst = KT[seg][row:row + 128, t0:t1], R_KT[seg]
                                elif kind == "Q":
                                    dst, rdst = QT[seg][row:row + 128, t0 - WOFF:t1 - WOFF], R_QT[seg]
                                elif kind == "GA":
                                    dst, rdst = GA[seg][row:row + 128, t0 - WOFF:t1 - WOFF], R_GA[seg]
                                elif kind == "UB":
                                    dst, rdst = UB[seg][row:row + 128, t0 - WOFF:t1 - WOFF], R_UB[seg]
                                else:
                                    dst, rdst = GB[seg][row:row + 128, t0 - WOFF:t1 - WOFF], R_GB[seg]
                                em.dma("sp", [(dst, sb[:, 0:ntok])], dsb, reads=[rsb], writes=[rdst], accum=True)
                        if nxt_tiles and li >= 2:
                            ensure_tiles(p + 1, [nxt_tiles.pop(0)])
                    while nxt_tiles:
                        ensure_tiles(p + 1, [nxt_tiles.pop(0)])

        if "B" in phases:
            with phase() as ph:
                QM = SEGS[0].nQ
                WM = SEGS[0].W
                NTN = SEGS[0].nqt + 4
                WS = WM // 16
                QS = QM // 16
                kth = Ring(em, ph, nc, "kth", 2, [128, WM], BF16, dsem=True)
                vh = Ring(em, ph, nc, "vh", 2, [128, NTN, 128], BF16, dsem=True)
                qth = Ring(em, ph, nc, "qth", 2, [128, QM], BF16, dsem=True)
                gah = Ring(em, ph, nc, "gah", 1, [128, QM], BF16, dsem=True)
                KR = Ring(em, ph, nc, "KR", 1, [128, 16, WS], BF16)
                QR = Ring(em, ph, nc, "QR", 1, [128, 16, QS], BF16)
                v3r = Ring(em, ph, nc, "v3r", 1, [128, 16, 3, 128], BF16, dsem=True)
                hk12 = Ring(em, ph, nc, "hk12", 2, [128, 5, 128], BF16, dsem=True)
                eb12 = Ring(em, ph, nc, "eb12", 2, [128, 5, 128], BF16)
                hk3 = Ring(em, ph, nc, "hk3", 2, [128, 2, 128], BF16, dsem=True)
                eb3 = Ring(em, ph, nc, "eb3", 2, [128, 2, 128], BF16)
                ogst = Ring(em, ph, nc, "ogst", 1, [128, QM], BF16, dsem=True)
                sgr = Ring(em, ph, nc, "sgr", 2, [128, QM], BF16)
                pr = Ring(em, ph, nc, "pr", 6, [128, 512], BF16)
                accN = ph.enter_context(nc.sbuf_tensor("accN", [128, QM], F32))
                accD = ph.enter_context(nc.sbuf_tensor("accD", [128, QM], F32))
                tmpf = ph.enter_context(nc.sbuf_tensor("tmpf", [128, QM], F32))
                R_accN, R_accD, R_tmpf = Res(), Res(), Res()
                sbank = Ring(em, ph, nc, "sbank", 4, [128, 512], F32, psum=True)
                numb = Ring(em, ph, nc, "numb", 2, [128, 512], F32, psum=True)
                denb = Ring(em, ph, nc, "denb", 2, [128, 512], F32, psum=True)
                pob = sbank
                onesv = ph.enter_context(nc.sbuf_tensor("onesv", [128, NTN, 128], BF16))
                vldt = ph.enter_context(nc.sbuf_tensor("vldt", [128, NTN], F32))
                ones3 = ph.enter_context(nc.sbuf_tensor("ones3", [128, 16, 3, 128], BF16))
                vld3 = ph.enter_context(nc.sbuf_tensor("vld3", [128, 16, 3], F32))
                R_onesv, R_vldt, R_ones3, R_vld3 = Res(), Res(), Res(), Res()
                d_b = em.dsem()

                ub = Ring(em, ph, nc, "ub", 1, [128, QM], F32, dsem=True)
                gbr = Ring(em, ph, nc, "gbr", 1, [128, QM], BF16, dsem=True)
                sA = ph.enter_context(nc.sbuf_tensor("sA", [128, QM], F32))
                sB = ph.enter_context(nc.sbuf_tensor("sB", [128, QM], F32))
                R_sA, R_sB = Res(), Res()
                plr = Ring(em, ph, nc, "plr", 2, [128, QM], BF16)
                icb = ph.enter_context(nc.sbuf_tensor("icb", [128, QM], F32))
                R_icb = Res()
                pw = ph.enter_context(nc.sbuf_tensor("pw", [128, 4, 2, 256], BF16))
                R_pw = Res()
                psc = ph.enter_context(nc.sbuf_tensor("psc", [128, 8], F32))
                R_psc = Res()
                sgb2 = Ring(em, ph, nc, "sgb2", 1, [128, QM], BF16)
                obt = Ring(em, ph, nc, "obt", 2, [128, 512], F32)
                ogst2 = Ring(em, ph, nc, "ogst2", 1, [128, QM], BF16, dsem=True)
                d_p = em.dsem()
                em.dma("pool", [(pw[:], pool_w.rearrange("g (ic p) o -> p g ic o", p=128))], em.dsem(), writes=[R_pw])
                em.dma("sp", [(psc[:], pscale_d)], em.dsem(), writes=[R_psc])
                for buf, rr in ((sA, R_sA), (sB, R_sB)):
                    em.op("pool", lambda e, buf=buf: e.memset(buf[:], 0.0), writes=[rr])
                for i in range(2):
                    b_, r_, _ = plr.next()
                    em.op("pool", lambda e, b_=b_: e.memset(b_[:], 0.0), writes=[r_])

                def pool_items(seg):
                    sg = SEGS[seg]
                    n = sg.nQ
                    for g in range(4):
                        w = POOL_WINDOWS[g]
                        em.dma("sp", [(icb[:, 0:n], bass.AP(icnt[seg].tensor, g * n, [[0, 128], [1, n]]))], d_p, writes=[R_icb])
                        yield
                        pls = []
                        for ic in range(2):
                            u, ru, du = ub.next()
                            r0 = 256 * g + 128 * ic
                            em.dma("sp", [(u[:, 0:n], UB[seg][r0:r0 + 128, :])], du, reads=[R_UB[seg]], writes=[ru])
                            yield
                            em.op("dve", lambda e, u=u: e.tensor_tensor(sA[:, 1:n], u[:, 0:n - 1], u[:, 1:n], ALU.add), reads=[ru], writes=[R_sA])
                            yield
                            cur, rcur, oth, roth = sA, R_sA, sB, R_sB
                            lo, hi = 1, n
                            step = 1
                            ww = 2
                            while ww < w:
                                nlo, nhi = lo + step, hi - step
                                em.op("dve", lambda e, cur=cur, oth=oth, nlo=nlo, nhi=nhi, step=step: e.tensor_tensor(
                                    oth[:, nlo:nhi], cur[:, nlo - step:nhi - step], cur[:, nlo + step:nhi + step], ALU.add),
                                    reads=[rcur], writes=[roth])
                                yield
                                cur, rcur, oth, roth = oth, roth, cur, rcur
                                lo, hi = nlo, nhi
                                step *= 2
                                ww *= 2
                            em.op("dve", lambda e, cur=cur, oth=oth: e.tensor_tensor(oth[:, 8:n - 8], cur[:, 8:n - 8], icb[:, 8:n - 8], ALU.mult),
                                  reads=[rcur, R_icb], writes=[roth])
                            yield
                            pl, rpl, _ = plr.next()
                            em.op("dve", lambda e, oth=oth, u=u, pl=pl: e.tensor_tensor(pl[:, 8:n - 8], oth[:, 8:n - 8], u[:, 8:n - 8], ALU.subtract),
                                  reads=[roth, ru], writes=[rpl])
                            yield
                            pls.append((pl, rpl))
                            yield
                        for oc in range(2):
                            r0 = 256 * g + 128 * oc
                            gbb, rgb, dgb = gbr.next()
                            em.dma("sp", [(gbb[:, 0:n], GB[seg][r0:r0 + 128, :])], dgb, reads=[R_GB[seg]], writes=[rgb])
                            yield
                            s2, rs2, _ = sgb2.next()
                            em.op("act", lambda e, s2=s2, gbb=gbb: e.activation(s2[:, 0:n], gbb[:, 0:n], AF.Silu), reads=[rgb], writes=[rs2])
                            yield
                            og2, rog2, dog2 = ogst2.next()
                            for (off, nn) in split_blocks(n):
                                bank, rbank, _ = pob.next()
                                em.group("pe", [(lambda e, ic=ic, bank=bank, off=off, nn=nn: e.matmul(bank[:, 0:nn], pw[:, g, ic, 128 * oc:128 * oc + 128], pls[ic][0][:, off:off + nn],
                                                                                         start=(ic == 0), stop=(ic == 1))) for ic in range(2)],
                                         reads=[pls[0][1], pls[1][1], R_pw], writes=[rbank])
                                ob, rob, _ = obt.next()
                                em.op("act", lambda e, ob=ob, bank=bank, nn=nn, ch=2 * g + oc: e.activation(ob[:, 0:nn], bank[:, 0:nn], AF.Identity, scale=psc[:, ch:ch + 1]),
                                      reads=[rbank, R_psc], writes=[rob])
                                yield
                                em.op("dve", lambda e, ob=ob, off=off, nn=nn, og2=og2, s2=s2: e.tensor_tensor(og2[:, off:off + nn], ob[:, 0:nn], s2[:, off:off + nn], ALU.mult),
                                      reads=[rob, rs2], writes=[rog2])
                                yield
                            em.dma("sp", [(OG[seg][2048 + r0:2048 + r0 + 128, :], og2[:, 0:n])], dog2, reads=[rog2], writes=[R_OG[seg]], accum=True)
                            yield

                def sb_ap(t, off, dims):
                    return bass.AP(t, off, [list(t[:].ap[0])] + [list(d_) for d_ in dims])

                for seg in range(2):
                    sg = SEGS[seg]
                    ntn = sg.nqt + 4
                    ws = sg.W // 16
                    qs_n = sg.nQ // 16
                    pitems = pool_items(seg)
                    pcount = [0]
                    prate = max(1, (sg.nqt * 2 * NH) // 230)
                    k3t = [(k3, 128 * k3, min(128, ws - 128 * k3)) for k3 in range((ws + 127) // 128)]
                    em.dma("sp", [(vldt[:, 0:ntn], vld[seg][768:768 + 128 * ntn].rearrange("(t j) -> j t", j=128))], d_b, writes=[R_vldt],
                           allow_slow_non_contiguous=True)
                    em.op("dve", lambda e, ntn=ntn: e.tensor_copy(onesv[:, 0:ntn, :], vldt[:, 0:ntn].unsqueeze(2).to_broadcast([128, ntn, 128])),
                          reads=[R_vldt], writes=[R_onesv])
                    em.op("pool", lambda e: e.memset(vld3[:], 0.0), writes=[R_vld3])
                    em.dma("sp", [(vld3[0:rows, :, k3], bass.AP(vld[seg].tensor, 16 * st_, [[16, rows], [1, 16]])) for (k3, st_, rows) in k3t], d_b,
                           writes=[R_vld3], allow_slow_non_contiguous=True)
                    em.op("dve", lambda e: e.tensor_copy(ones3[:], vld3[:].unsqueeze(3).to_broadcast([128, 16, 3, 128])),
                          reads=[R_vld3], writes=[R_ones3])

                    def load_main(h, seg=seg, sg=sg, ntn=ntn):
                        kb, rk, dk_ = kth.next()
                        em.dma("sp", [(kb[:, 0:sg.W], KT[seg][128 * h:128 * h + 128, :])], dk_, reads=[R_KT[seg]], writes=[rk])
                        qb, rq, dq = qth.next()
                        em.dma("sp", [(qb[:, 0:sg.nQ], QT[seg][128 * h:128 * h + 128, :])], dq, reads=[R_QT[seg]], writes=[rq])
                        h12, rh12, dh12 = hk12.next()
                        em.dma("sp", [(h12[:], bass.AP(VECR, h * NV, [[1, 128], [128, 5], [1, 128]]))], dh12, reads=[R_VECR], writes=[rh12])
                        h3, rh3, dh3 = hk3.next()
                        em.dma("sp", [(h3[:], bass.AP(VECR, h * NV + NV12, [[1, 128], [128, 2], [1, 128]]))], dh3, reads=[R_VECR], writes=[rh3])
                        vb, rv, dv = vh.next()
                        em.dma("sp", [(vb[:, 0:ntn, :], VS[seg][768:768 + 128 * ntn, 128 * h:128 * h + 128].rearrange("(t j) d -> j t d", j=128))], dv,
                               reads=[R_VS[seg]], writes=[rv])
                        gb_, rg, dg = gah.next()
                        em.dma("sp", [(gb_[:, 0:sg.nQ], GA[seg][128 * h:128 * h + 128, :])], dg, reads=[R_GA[seg]], writes=[rg])
                        return (kb, rk, vb, rv, qb, rq, gb_, rg, h12, rh12, h3, rh3)

                    def load_v3(h, seg=seg, k3t=k3t):
                        v3, rv3, dv3 = v3r.next()
                        em.dma("sp", [(v3[0:rows, :, k3, :], bass.AP(VS[seg].tensor, (16 * st_) * D + 128 * h, [[16 * D, rows], [D, 16], [1, 128]]))
                                      for (k3, st_, rows) in k3t], dv3, reads=[R_VS[seg]], writes=[rv3])
                        return (v3, rv3)

                    def prep_steps(hd):
                        kb, rk, vb, rv, qb, rq, gb_, rg, h12, rh12, h3, rh3 = hd
                        e12, re12, _ = eb12.next()
                        e3, re3, _ = eb3.next()
                        kr_, rkr, _ = KR.next()
                        qr_, rqr, _ = QR.next()
                        sgb, rsg, _ = sgr.next()
                        steps = []
                        steps.append(lambda: em.op("act", lambda e: e.activation(e12[:], sb_ap(h12, 127, [[128, 5], [-1, 128]]), AF.Copy), reads=[rh12], writes=[re12]))
                        steps.append(lambda: em.op("act", lambda e: e.activation(e3[:], sb_ap(h3, 127, [[128, 2], [-1, 128]]), AF.Copy), reads=[rh3], writes=[re3]))
                        for g4 in range(4):
                            steps.append(lambda g4=g4: em.op("dve", lambda e: e.tensor_copy(kr_[:, 4 * g4:4 * g4 + 4, 0:ws], sb_ap(kb, 4 * g4, [[1, 4], [16, ws]])),
                                                             reads=[rk], writes=[rkr]))
                        for g8 in range(2):
                            steps.append(lambda g8=g8: em.op("dve", lambda e: e.tensor_copy(qr_[:, 8 * g8:8 * g8 + 8, 0:qs_n], sb_ap(qb, 8 * g8, [[1, 8], [16, qs_n]])),
                                                             reads=[rq], writes=[rqr]))
                        steps.append(lambda: em.op("act", lambda e: e.activation(sgb[:, 0:sg.nQ], gb_[:, 0:sg.nQ], AF.Silu), reads=[rg], writes=[rsg]))
                        return steps, (e12, re12, e3, re3, kr_, rkr, qr_, rqr, sgb, rsg)

                    hd_cur = load_main(0)
                    v3_cur = load_v3(0)
                    st0, pp_cur = prep_steps(hd_cur)
                    for st_fn in st0:
                        st_fn()
                    for h in range(NH):
                        kb, rk, vb, rv, qb, rq, gb_, rg, h12, rh12, h3, rh3 = hd_cur
                        v3, rv3 = v3_cur
                        e12, re12, e3, re3, kr_, rkr, qr_, rqr, sgb, rsg = pp_cur
                        nsteps = []
                        if h + 1 < NH:
                            hd_nxt = load_main(h + 1)
                        nxt_box = {}
                        ogb, rog, dog = ogst.next()

                        units = []
                        for (q0_, nq) in [(o_, min(128, qs_n - o_)) for o_ in range(0, qs_n, 128)]:
                            ng = max(1, min(16, 256 // nq))
                            for g0 in range(0, 16, ng):
                                units.append((q0_, nq, g0, min(ng, 16 - g0)))
                        items = [("r", u_i, q0_, nq, g0, ng) for u_i, (q0_, nq, g0, ng) in enumerate(units)]
                        items.append(("prefetch",))
                        for m in range(sg.nqt):
                            items.append(("n", m, 0, 3))
                            items.append(("n", m, 3, 2))
                        nd_state = {}

                        def emit_qk(it):
                            sb_, rsb_, _ = sbank.next()
                            pb, rpb, _ = pr.next()
                            if it[0] == "n":
                                _, m, k0, nk = it
                                em.group("pe", [(lambda e, kk=kk: e.matmul(sb_[:, 128 * kk:128 * kk + 128], kb[:, 128 * (m + 6 + k0 + kk):128 * (m + 6 + k0 + kk) + 128],
                                                                        qb[:, 128 * m:128 * m + 128], start=True, stop=True))
                                                for kk in range(nk)], reads=[rk, rq], writes=[rsb_])
                                em.op("act", lambda e: e.activation(pb[:, 0:128 * nk], sb_[:, 0:128 * nk], AF.Exp, scale=SCALE), reads=[rsb_], writes=[rpb])
                                em.op("dve", lambda e: e.tensor_tensor(pb[:, 0:128 * nk], pb[:, 0:128 * nk], e12[:, k0:k0 + nk, :].rearrange("p a b -> p (a b)"), ALU.mult),
                                      reads=[rpb, re12], writes=[rpb])
                            else:
                                _, u_i, q0_, nq, g0, ng = it
                                tiles = [(k3, st_, rows) for (k3, st_, rows) in k3t if st_ in (q0_, q0_ + 128)]
                                fns = []
                                for gi in range(ng):
                                    g = g0 + gi
                                    for ti, (k3, st_, rows) in enumerate(tiles):
                                        c0 = (ti * ng + gi) * nq
                                        fns.append(lambda e, g=g, st_=st_, rows=rows, c0=c0: e.matmul(sb_[0:rows, c0:c0 + nq], kr_[:, g, st_:st_ + rows],
                                                                                                  qr_[:, g, q0_:q0_ + nq], start=True, stop=True))
                                em.group("pe", fns, reads=[rkr, rqr], writes=[rsb_])
                                if len(tiles) == 2 and tiles[0][2] == 128 and tiles[1][2] == 128:
                                    em.op("act", lambda e: e.activation(pb[:, 0:2 * ng * nq], sb_[:, 0:2 * ng * nq], AF.Exp, scale=SCALE), reads=[rsb_], writes=[rpb])
                                    em.op("dve", lambda e: e.tensor_tensor(sb_ap(pb, 0, [[ng * nq, 2], [nq, ng], [1, nq]]), sb_ap(pb, 0, [[ng * nq, 2], [nq, ng], [1, nq]]),
                                                                           sb_ap(e3, 0, [[128, 2], [0, ng], [1, nq]]), ALU.mult), reads=[rpb, re3], writes=[rpb])
                                    tiles_e = []
                                else:
                                    tiles_e = tiles
                                for ti, (k3, st_, rows) in enumerate(tiles_e):
                                    em.op("act", lambda e, ti=ti, rows=rows: e.activation(pb[0:rows, ti * ng * nq:(ti + 1) * ng * nq], sb_[0:rows, ti * ng * nq:(ti + 1) * ng * nq],
                                                                                        AF.Exp, scale=SCALE), reads=[rsb_], writes=[rpb])
                                    em.op("dve", lambda e, ti=ti, rows=rows: e.tensor_tensor(sb_ap(pb, ti * ng * nq, [[nq, ng], [1, nq]])[0:rows], sb_ap(pb, ti * ng * nq, [[nq, ng], [1, nq]])[0:rows],
                                                                                           sb_ap(e3, 128 * ti, [[0, ng], [1, nq]])[0:rows], ALU.mult),
                                          reads=[rpb, re3], writes=[rpb])
                            return (it, pb, rpb)

                        def emit_pv(it, pb, rpb):
                            if it[0] == "n":
                                _, m, k0, nk = it
                                slot = m % 4
                                if k0 == 0:
                                    if slot == 0:
                                        nd_state["n"] = (numb.next(), denb.next())
                                (nu, rnu, _), (de_, rde, _) = nd_state["n"]
                                fns = []
                                for kk in range(nk):
                                    kt = k0 + kk
                                    fns.append(lambda e, kk=kk, kt=kt: e.matmul(nu[:, 128 * slot:128 * slot + 128], vb[:, m + kt, :], pb[:, 128 * kk:128 * kk + 128],
                                                                           start=(kt == 0), stop=(kt == 4)))
                                for kk in range(nk):
                                    kt = k0 + kk
                                    fns.append(lambda e, kk=kk, kt=kt: e.matmul(de_[:, 128 * slot:128 * slot + 128], onesv[:, m + kt, :], pb[:, 128 * kk:128 * kk + 128],
                                                                           start=(kt == 0), stop=(kt == 4)))
                                em.group("pe", fns, reads=[rv, rpb, R_onesv], writes=[rnu, rde])
                                if k0 == 3 and (slot == 3 or m == sg.nqt - 1):
                                    m0 = m - slot
                                    wdt = 128 * (slot + 1)
                                    em.op("dve", lambda e: e.tensor_tensor(accN[:, 128 * m0:128 * m0 + wdt], accN[:, 128 * m0:128 * m0 + wdt], nu[:, 0:wdt], ALU.add),
                                          reads=[rnu, R_accN], writes=[R_accN])
                                    em.op("dve", lambda e: e.tensor_tensor(accD[:, 128 * m0:128 * m0 + wdt], accD[:, 128 * m0:128 * m0 + wdt], de_[:, 0:wdt], ALU.add),
                                          reads=[rde, R_accD], writes=[R_accD])
                                    c0_, c1_ = 128 * m0, 128 * m0 + wdt
                                    em.op("act", lambda e: e.activation(tmpf[:, c0_:c1_], accD[:, c0_:c1_], AF.Ln), reads=[R_accD], writes=[R_tmpf])
                                    em.op("act", lambda e: e.activation(tmpf[:, c0_:c1_], tmpf[:, c0_:c1_], AF.Exp, scale=-1.0), reads=[R_tmpf], writes=[R_tmpf])
                                    em.op("dve", lambda e: e.tensor_tensor(tmpf[:, c0_:c1_], accN[:, c0_:c1_], tmpf[:, c0_:c1_], ALU.mult), reads=[R_accN, R_tmpf], writes=[R_tmpf])
                                    em.op("dve", lambda e: e.tensor_tensor(ogb[:, c0_:c1_], tmpf[:, c0_:c1_], sgb[:, c0_:c1_], ALU.mult), reads=[R_tmpf, rsg], writes=[rog])
                            else:
                                _, u_i, q0_, nq, g0, ng = it
                                tiles = [(k3, st_, rows) for (k3, st_, rows) in k3t if st_ in (q0_, q0_ + 128)]
                                (nu, rnu, _), (de_, rde, _) = (numb.next(), denb.next())
                                fns = []
                                for gi in range(ng):
                                    g = g0 + gi
                                    col = gi * nq
                                    for ti, (k3, st_, rows) in enumerate(tiles):
                                        c0 = (ti * ng + gi) * nq
                                        fns.append(lambda e, g=g, k3=k3, rows=rows, c0=c0, col=col, ti=ti: e.matmul(nu[:, col:col + nq], v3[0:rows, g, k3, :], pb[0:rows, c0:c0 + nq],
                                                                                                                start=(ti == 0), stop=(ti == len(tiles) - 1)))
                                    for ti, (k3, st_, rows) in enumerate(tiles):
                                        c0 = (ti * ng + gi) * nq
                                        fns.append(lambda e, g=g, k3=k3, rows=rows, c0=c0, col=col, ti=ti: e.matmul(de_[:, col:col + nq], ones3[0:rows, g, k3, :], pb[0:rows, c0:c0 + nq],
                                                                                                                start=(ti == 0), stop=(ti == len(tiles) - 1)))
                                em.group("pe", fns, reads=[rv3, rpb, R_ones3], writes=[rnu, rde])
                                dstn = sb_ap(accN, 16 * q0_ + g0, [[16, nq], [1, ng]])
                                em.op("act", lambda e: e.activation(dstn, sb_ap(nu, 0, [[1, nq], [nq, ng]]), AF.Copy), reads=[rnu], writes=[R_accN])
                                dstd = sb_ap(accD, 16 * q0_ + g0, [[16, nq], [1, ng]])
                                em.op("dve", lambda e: e.tensor_copy(dstd, sb_ap(de_, 0, [[1, nq], [nq, ng]])), reads=[rde], writes=[R_accD])

                        LAG = 4
                        pend = []
                        for it in items:
                            if it[0] == "prefetch":
                                while pend:
                                    emit_pv(*pend.pop(0))
                                if h + 1 < NH:
                                    v3_nxt = load_v3(h + 1)
                                    nsteps, pp_nxt = prep_steps(hd_nxt)
                                continue
                            pend.append(emit_qk(it))
                            if len(pend) > LAG:
                                emit_pv(*pend.pop(0))
                            if it[0] == "n" and it[2] == 3 and nsteps:
                                nsteps.pop(0)()
                            if it[0] == "n":
                                pcount[0] += 1
                                if pcount[0] % prate == 0:
                                    next(pitems, None)
                        while pend:
                            emit_pv(*pend.pop(0))
                        n_ = sg.nQ
                        em.dma("sp", [(OG[seg][128 * h:128 * h + 128, :], ogb[:, 0:n_])], dog, reads=[rog], writes=[R_OG[seg]], accum=True)
                        while nsteps:
                            nsteps.pop(0)()
                        if h + 1 < NH:
                            hd_cur, v3_cur, pp_cur = hd_nxt, v3_nxt, pp_nxt
                    for _ in pitems:
                        pass

        def ln_epilogue(lb, M, banks, rbanks, xres, rxres, out_t, rout):
            ztr, junkr, statr, gbt, rgb_, bbt, rbb = lb
            zt, rzt, _ = ztr.next()
            junk, rjunk, _ = junkr.next()
            stat, rstat, _ = statr.next()
            for s4 in range(4):
                em.op("dve", lambda e, s4=s4: e.scalar_tensor_tensor(zt[0:M, 512 * s4:512 * s4 + 512], xres[0:M, 512 * s4:512 * s4 + 512], ALPHA,
                                                                   banks[s4][0:M, :], ALU.mult, ALU.add),
                      reads=[rxres, rbanks[s4]], writes=[rzt])
            em.op("dve", lambda e: e.reduce_sum(stat[0:M, 0:1], zt[0:M, :], AX.X), reads=[rzt], writes=[rstat])
            em.op("dve", lambda e: e.tensor_scalar(stat[0:M, 1:2], stat[0:M, 0:1], -1.0 / D, None, ALU.mult), reads=[rstat], writes=[rstat])
            em.op("act", lambda e: e.activation(junk[0:M, :], zt[0:M, :], AF.Square, bias=stat[0:M, 1:2], scale=1.0, accum_out=stat[0:M, 2:3]),
                  reads=[rzt, rstat], writes=[rjunk, rstat])
            em.op("dve", lambda e: e.tensor_scalar(stat[0:M, 3:4], stat[0:M, 2:3], 1.0 / D, LN_EPS, ALU.mult, ALU.add), reads=[rstat], writes=[rstat])
            em.op("act", lambda e: e.activation(stat[0:M, 6:7], stat[0:M, 3:4], AF.Sqrt), reads=[rstat], writes=[rstat])
            em.op("dve", lambda e: e.reciprocal(stat[0:M, 4:5], stat[0:M, 6:7]), reads=[rstat], writes=[rstat])
            em.op("dve", lambda e: e.tensor_tensor(stat[0:M, 5:6], stat[0:M, 1:2], stat[0:M, 4:5], ALU.mult), reads=[rstat], writes=[rstat])
            em.op("act", lambda e: e.activation(out_t[0:M, :], zt[0:M, :], AF.Identity, bias=stat[0:M, 5:6], scale=stat[0:M, 4:5]),
                  reads=[rzt, rstat], writes=[rout])
            em.op("dve", lambda e: e.tensor_tensor(out_t[0:M, :], out_t[0:M, :], gbt[0:M, :], ALU.mult), reads=[rout, rgb_], writes=[rout])
            em.op("pool", lambda e: e.tensor_tensor(out_t[0:M, :], out_t[0:M, :], bbt[0:M, :], ALU.add), reads=[rout, rbb], writes=[rout])

        def ln_bufs(ph, layer):
            ztr = Ring(em, ph, nc, "zt%d_" % layer, 2, [128, D], F32)
            junkr = Ring(em, ph, nc, "junk%d_" % layer, 2, [128, D], BF16)
            statr = Ring(em, ph, nc, "stat%d_" % layer, 2, [128, 8], F32)
            gbt = ph.enter_context(nc.sbuf_tensor("gbt%d" % layer, [128, D], F32))
            bbt = ph.enter_context(nc.sbuf_tensor("bbt%d" % layer, [128, D], F32))
            rgb_, rbb = Res(), Res()
            em.dma("sp", [(gbt[:], bass.AP(ln_g.tensor, layer * D, [[0, 128], [1, D]]))], em.dsem(), writes=[rgb_])
            em.dma("sp", [(bbt[:], bass.AP(ln_b.tensor, layer * D, [[0, 128], [1, D]]))], em.dsem(), writes=[rbb])
            return (ztr, junkr, statr, gbt, rgb_, bbt, rbb)

        if "C" in phases:
            with phase() as ph:
                wo = ph.enter_context(nc.sbuf_tensor("wo", [128, 24, D], BF16))
                R_wos = [Res() for _ in range(4)]
                for c in range(4):
                    em.dma("pool", [(wo[:, :, 512 * c:512 * c + 512], w_out_ab[:, 512 * c:512 * c + 512].rearrange("(k p) n -> p k n", p=128))], em.dsem(), writes=[R_wos[c]])
                lb = ln_bufs(ph, 0)
                ogt = Ring(em, ph, nc, "ogt", 2, [128, 24, 128], BF16, dsem=True)
                xres = Ring(em, ph, nc, "xres", 2, [128, D], F32, dsem=True)
                x1t = Ring(em, ph, nc, "x1t", 2, [128, D], F32, dsem=True)
                x1b = Ring(em, ph, nc, "x1b", 2, [128, D], BF16)
                x1Ts = Ring(em, ph, nc, "x1Ts", 2, [128, 16, 128], BF16, dsem=True)
                banks = [ph.enter_context(nc.psum_tensor("cb%d" % i, [128, 512], F32)) for i in range(4)]
                rbanks = [Res() for _ in range(4)]
                tr = Ring(em, ph, nc, "trC", 2, [128, 8, 128], BF16, psum=True)
                tiles = []
                for seg in range(2):
                    tiles += [(seg, "own", i) for i in range(SEGS[seg].nOwn // 128)] + [(seg, "edge", 0)]

                def c_load(t):
                    seg, kind, i = t
                    sg = SEGS[seg]
                    ob, rob, dob = ogt.next()
                    xr, rxr, dxr = xres.next()
                    if kind == "own":
                        q0 = E + 128 * i
                        em.dma("sp", [(ob[:, :, :], OG[seg][:, q0:q0 + 128].rearrange("(c p) q -> p c q", p=128))], dob, reads=[R_OG[seg]], writes=[rob])
                        em.dma("sp", [(xr[:, :], xw[seg][WOFF + q0:WOFF + q0 + 128, :])], dxr, writes=[rxr])
                    else:
                        qa, qb_ = E - 1, E + sg.nOwn
                        em.dma("sp", [(ob[:, :, 0:1], OG[seg][:, qa:qa + 1].rearrange("(c p) q -> p c q", p=128)),
                                      (ob[:, :, 1:2], OG[seg][:, qb_:qb_ + 1].rearrange("(c p) q -> p c q", p=128))], dob,
                               reads=[R_OG[seg]], writes=[rob], allow_slow_non_contiguous=True)
                        em.dma("sp", [(xr[0:1, :], xw[seg][WOFF + qa:WOFF + qa + 1, :]), (xr[1:2, :], xw[seg][WOFF + qb_:WOFF + qb_ + 1, :])], dxr, writes=[rxr])
                    return (ob, rob, xr, rxr)

                def c_mm(t, hd):
                    ob, rob, xr, rxr = hd
                    M = 128 if t[1] == "own" else 2
                    for s4 in range(4):
                        em.group("pe", [(lambda e, k=k, s4=s4: e.matmul(banks[s4][0:M, :], ob[:, k, 0:M], wo[:, k, 512 * s4:512 * s4 + 512],
                                                                        start=(k == 0), stop=(k == 23))) for k in range(24)],
                                 reads=[rob, R_wos[s4]], writes=[rbanks[s4]])

                def c_epi(t, hd):
                    seg, kind, i = t
                    sg = SEGS[seg]
                    ob, rob, xr, rxr = hd
                    M = 128 if kind == "own" else 2
                    xo, rxo, dxo = x1t.next()
                    ln_epilogue(lb, M, banks, rbanks, xr, rxr, xo, rxo)
                    if kind == "own":
                        em.dma("pool", [(X1[seg][1 + 128 * i:1 + 128 * i + 128, :], xo[:, :])], dxo, reads=[rxo], writes=[R_X1[seg]], accum=True)
                    else:
                        em.dma("pool", [(X1[seg][0:1, :], xo[0:1, :]), (X1[seg][sg.nOwn + 1:sg.nOwn + 2, :], xo[1:2, :])], dxo, reads=[rxo], writes=[R_X1[seg]], accum=True)
                    xbb, rxbb, _ = x1b.next()
                    em.op("act", lambda e: e.activation(xbb[0:M, :], xo[0:M, :], AF.Copy), reads=[rxo], writes=[rxbb])
                    return (xbb, rxbb)

                def c_tr(t, cast):
                    seg, kind, i = t
                    sg = SEGS[seg]
                    xbb, rxbb = cast
                    M = 128 if kind == "own" else 2
                    xs, rxs, dxs = x1Ts.next()
                    for g in range(2):
                        pt, rpt, _ = tr.next()
                        em.group("pe", [(lambda e, c=c, pt=pt: e.transpose(pt[:, c % 8, 0:M], xbb[0:M, c * 128:(c + 1) * 128], ident[0:M, 0:M]))
                                        for c in range(8 * g, 8 * g + 8)], reads=[rxbb, R_ident], writes=[rpt])
                        em.op("dve", lambda e, g=g, pt=pt: e.tensor_copy(xs[:, 8 * g:8 * g + 8, 0:M], pt[:, :, 0:M]), reads=[rpt], writes=[rxs])
                    if kind == "own":
                        em.dma("pool", [(X1T[seg][:, 1 + 128 * i:1 + 128 * i + 128].rearrange("(c p) q -> p c q", p=128), xs[:, :, :])], dxs,
                               reads=[rxs], writes=[R_X1T[seg]], accum=True)
                    else:
                        em.dma("pool", [(X1T[seg][:, 0:1].rearrange("(c p) q -> p c q", p=128), xs[:, :, 0:1]),
                                      (X1T[seg][:, sg.nOwn + 1:sg.nOwn + 2].rearrange("(c p) q -> p c q", p=128), xs[:, :, 1:2])], dxs,
                               reads=[rxs], writes=[R_X1T[seg]], allow_slow_non_contiguous=True, accum=True)

                hd_next = c_load(tiles[0])
                prev = None
                for ti, t in enumerate(tiles):
                    hd = hd_next
                    if ti + 1 < len(tiles):
                        hd_next = c_load(tiles[ti + 1])
                    c_mm(t, hd)
                    if prev is not None:
                        c_tr(*prev)
                    cast = c_epi(t, hd)
                    prev = (t, cast)
                c_tr(*prev)

        if "D" in phases:
            with phase() as ph:
                ND0, ND1 = SEGS[0].nD, SEGS[1].nD
                x1T = ph.enter_context(nc.sbuf_tensor("x1Tr", [128, 16, ND0 + ND1], BF16))
                R_x1T = [Res(), Res()]
                d_x = em.dsem(ph)
                segoff = [0, ND0]
                R_x1b = {}
                for seg in range(2):
                    n = SEGS[seg].nD
                    for (off, nn) in split_blocks(n, 512, 2):
                        R_x1b[(seg, off)] = Res()
                        em.dma("sp", [(x1T[:, :, segoff[seg] + off:segoff[seg] + off + nn], X1T[seg][:, off:off + nn].rearrange("(c p) q -> p c q", p=128))],
                               em.dsem(), reads=[R_X1T[seg]], writes=[R_x1b[(seg, off)]])
                slab = Ring(em, ph, nc, "slabD", 2, [128, 4, 16, 128], BF16, dsem=True)
                cw = ph.enter_context(nc.sbuf_tensor("cw", [128, 16, 3], F32))
                fl = ph.enter_context(nc.sbuf_tensor("fl", [128, 4], F32))
                R_cw, R_fl = Res(), Res()
                em.dma("sp", [(cw[:], convw_d), (fl[:], flags)], d_x, writes=[R_cw, R_fl])
                cful = Ring(em, ph, nc, "cful", 2, [128, ND0], F32)
                tful = Ring(em, ph, nc, "tful", 2, [128, ND0], F32)
                accr = Ring(em, ph, nc, "accr", 2, [128, SEGS[0].nOwn], F32)
                yst = Ring(em, ph, nc, "yst", 2, [128, SEGS[0].nOwn], BF16, dsem=True)
                vsr = Ring(em, ph, nc, "vsr", 2, [128, 512], F32)
                sgt = Ring(em, ph, nc, "sgt", 2, [128, 512], F32)
                pb4 = [Ring(em, ph, nc, "pd%d" % j, 2, [128, 512], F32, psum=True) for j in range(4)]

                def load_slabD(f):
                    b, rb, db = slab.next()
                    em.dma("pool", [(b[:, j, :, :], w_in_c[:, 2048 * j + 128 * f:2048 * j + 128 * f + 128].rearrange("(k p) n -> p k n", p=128)) for j in range(4)],
                           db, writes=[rb])
                    return b, rb

                nxt = load_slabD(0)
                for f in range(16):
                    b, rb = nxt
                    if f + 1 < 16:
                        nxt = load_slabD(f + 1)
                    for seg in range(2):
                        sg = SEGS[seg]
                        n = sg.nD
                        cf, rcf, _ = cful.next()
                        tf, rtf, _ = tful.next()
                        for (off, nn) in split_blocks(n, 512, 2):
                            col = segoff[seg] + off
                            bk = [pb4[j].next() for j in range(4)]
                            for j in range(4):
                                em.group("pe", [(lambda e, k=k, j=j, col=col, nn=nn, bank=bk[j][0]: e.matmul(bank[:, 0:nn], b[:, j, k, :], x1T[:, k, col:col + nn],
                                                                                                start=(k == 0), stop=(k == 15))) for k in range(16)],
                                         reads=[R_x1b[(seg, off)], rb], writes=[bk[j][1]])
                            vs_, rvs, _ = vsr.next()
                            em.op("act", lambda e, vs_=vs_, nn=nn, bank=bk[2][0]: e.activation(vs_[:, 0:nn], bank[:, 0:nn], AF.Copy), reads=[bk[2][1]], writes=[rvs])
                            em.op("dve", lambda e, vs_=vs_, nn=nn, off=off, cf=cf, bank=bk[1][0]: e.tensor_tensor(cf[:, off:off + nn], bank[:, 0:nn], vs_[:, 0:nn], ALU.mult),
                                  reads=[bk[1][1], rvs], writes=[rcf])
                            sg_, rsg_, _ = sgt.next()
                            em.op("act", lambda e, sg_=sg_, nn=nn, bank=bk[3][0]: e.activation(sg_[:, 0:nn], bank[:, 0:nn], AF.Silu), reads=[bk[3][1]], writes=[rsg_])
                            em.op("dve", lambda e, sg_=sg_, nn=nn, off=off, tf=tf, bank=bk[0][0]: e.tensor_tensor(tf[:, off:off + nn], bank[:, 0:nn], sg_[:, 0:nn], ALU.mult),
                                  reads=[bk[0][1], rsg_], writes=[rtf])
                        no = sg.nOwn
                        em.op("pool", lambda e, cf=cf, seg=seg: e.tensor_scalar(cf[:, 0:1], cf[:, 0:1], fl[:, 2 * seg:2 * seg + 1], None, ALU.mult),
                              reads=[rcf, R_fl], writes=[rcf])
                        em.op("pool", lambda e, cf=cf, seg=seg, n=n: e.tensor_scalar(cf[:, n - 1:n], cf[:, n - 1:n], fl[:, 2 * seg + 1:2 * seg + 2], None, ALU.mult),
                              reads=[rcf, R_fl], writes=[rcf])
                        ac, rac, _ = accr.next()
                        em.op("pool", lambda e, ac=ac, cf=cf, no=no: e.tensor_scalar(ac[:, 0:no], cf[:, 0:no], cw[:, f, 0:1], None, ALU.mult),
                              reads=[rcf, R_cw], writes=[rac])
                        em.op("dve", lambda e, ac=ac, cf=cf, no=no: e.scalar_tensor_tensor(ac[:, 0:no], cf[:, 1:no + 1], cw[:, f, 1:2], ac[:, 0:no], ALU.mult, ALU.add),
                              reads=[rcf, R_cw, rac], writes=[rac])
                        em.op("dve", lambda e, ac=ac, cf=cf, no=no: e.scalar_tensor_tensor(ac[:, 0:no], cf[:, 2:no + 2], cw[:, f, 2:3], ac[:, 0:no], ALU.mult, ALU.add),
                              reads=[rcf, R_cw, rac], writes=[rac])
                        ys, rys, dys = yst.next()
                        em.op("pool", lambda e, ac=ac, tf=tf, ys=ys, no=no: e.tensor_tensor(ys[:, 0:no], ac[:, 0:no], tf[:, 1:no + 1], ALU.mult),
                              reads=[rac, rtf], writes=[rys])
                        em.dma("sp", [(YT[seg][128 * f:128 * f + 128, :], ys[:, 0:no])], dys, reads=[rys], writes=[R_YT[seg]], accum=True)

        if "E" in phases:
            with phase() as ph:
                wo = ph.enter_context(nc.sbuf_tensor("woc", [128, 16, D], BF16))
                R_wos = [Res() for _ in range(4)]
                for c in range(4):
                    em.dma("pool", [(wo[:, :, 512 * c:512 * c + 512], w_out_c[:, 512 * c:512 * c + 512].rearrange("(k p) n -> p k n", p=128))], em.dsem(), writes=[R_wos[c]])
                lb = ln_bufs(ph, 1)
                ytt = Ring(em, ph, nc, "ytt", 3, [128, 16, 128], BF16, dsem=True)
                xres = Ring(em, ph, nc, "xresE", 3, [128, D], F32, dsem=True)
                outt = Ring(em, ph, nc, "outt", 2, [128, D], F32, dsem=True)
                banks = [ph.enter_context(nc.psum_tensor("eb%d" % i, [128, 512], F32)) for i in range(4)]
                rbanks = [Res() for _ in range(4)]
                tiles = [(seg, i) for seg in range(2) for i in range(SEGS[seg].nOwn // 128)]

                def e_load(t):
                    seg, i = t
                    yb, ryb, dyb = ytt.next()
                    em.dma("sp", [(yb[:, :, :], YT[seg][:, 128 * i:128 * i + 128].rearrange("(c p) q -> p c q", p=128))], dyb, reads=[R_YT[seg]], writes=[ryb])
                    xr, rxr, dxr = xres.next()
                    em.dma("sp", [(xr[:, :], X1[seg][1 + 128 * i:1 + 128 * i + 128, :])], dxr, reads=[R_X1[seg]], writes=[rxr])
                    return (yb, ryb, xr, rxr)

                pre = [e_load(tiles[0]), e_load(tiles[1])]
                for ti, t in enumerate(tiles):
                    seg, i = t
                    yb, ryb, xr, rxr = pre.pop(0)
                    if ti + 2 < len(tiles):
                        pre.append(e_load(tiles[ti + 2]))
                    for s4 in range(4):
                        em.group("pe", [(lambda e, k=k, s4=s4: e.matmul(banks[s4][:, :], yb[:, k, :], wo[:, k, 512 * s4:512 * s4 + 512],
                                                                        start=(k == 0), stop=(k == 15))) for k in range(16)],
                                 reads=[ryb, R_wos[s4]], writes=[rbanks[s4]])
                    ot, rot, dot = outt.next()
                    ln_epilogue(lb, 128, banks, rbanks, xr, rxr, ot, rot)
                    em.dma("pool", [(yout[seg][128 * i:128 * i + 128, :], ot[:, :])], dot, reads=[rot], writes=[R_Y[seg]], accum=True)

        em.finish([R_Y[0], R_Y[1], R_KT[0], R_KT[1], R_VS[0], R_VS[1], R_QT[0], R_QT[1], R_GA[0], R_GA[1], R_UB[0], R_UB[1],
                   R_GB[0], R_GB[1], R_OG[0], R_OG[1], R_X1[0], R_X1[1], R_X1T[0], R_X1T[1], R_YT[0], R_YT[1], R_VECR])
    return nc


def make_in_maps(x_prompt, x_sample, rel_bias, w_in_ab, pool_w, pool_scale, w_out_ab, w_in_c, conv_w, w_out_c, ln_g, ln_b):
    f32 = np.float32
    x_prompt = np.asarray(x_prompt, f32)
    x_sample = np.asarray(x_sample, f32)
    shared = {
        "ident": np.eye(128, dtype=f32),
        "ohv": _onehot_table(),
        "rba": np.concatenate([np.asarray(rel_bias, f32), np.ones((1, NH), f32)], axis=0),
        "w_in_ab": np.ascontiguousarray(np.asarray(w_in_ab, f32)[0]),
        "pool_w": np.ascontiguousarray(np.asarray(pool_w, f32)[0]),
        "pscale": np.ascontiguousarray(np.asarray(pool_scale, f32)[0].reshape(8, 128).T),
        "w_out_ab": np.ascontiguousarray(np.asarray(w_out_ab, f32)[0]),
        "w_in_c": np.ascontiguousarray(np.asarray(w_in_c, f32)[0]),
        "convw": np.ascontiguousarray(np.asarray(conv_w, f32)[0].reshape(3, 16, 128).transpose(2, 1, 0)),
        "w_out_c": np.ascontiguousarray(np.asarray(w_out_c, f32)[0]),
        "ln_g": np.asarray(ln_g, f32),
        "ln_b": np.asarray(ln_b, f32),
    }
    maps = []
    for c in range(NCORES):
        m = dict(shared)
        fl = np.zeros((128, 4), f32)
        for s in range(2):
            sg = SEGS[s]
            if s == 0:
                seq, a, S = x_sample[0], 2048 * c, S_SAMPLE
            else:
                seq, a, S = x_prompt[c // 2], 1024 * (c % 2), S_PROMPT
            pos = a - E - WOFF + np.arange(sg.W)
            ok = (pos >= 0) & (pos < S)
            xwin = np.zeros((sg.W, D), f32)
            xwin[ok] = seq[pos[ok]]
            m["xw%d" % s] = xwin
            m["vld%d" % s] = ok.astype(f32)
            qpos = a - E + np.arange(sg.nQ)
            ic = np.zeros((4, sg.nQ), f32)
            for g, w in enumerate(POOL_WINDOWS):
                lo = np.maximum(qpos - w // 2, 0)
                hi = np.minimum(qpos + w // 2 - 1, S - 1)
                cnt = np.maximum(hi - lo + 1, 1)
                ic[g] = 1.0 / cnt
            m["icnt%d" % s] = ic
            fl[:, 2 * s] = 1.0 if a - 1 >= 0 else 0.0
            fl[:, 2 * s + 1] = 1.0 if a + sg.nOwn < S else 0.0
        m["flags"] = fl
        maps.append(m)
    return maps


_NC_CACHE = {}


def kernel(x_prompt, x_sample, rel_bias, w_in_ab, pool_w, pool_scale, w_out_ab, w_in_c, conv_w, w_out_c, ln_g, ln_b):
    if "nc" not in _NC_CACHE:
        _NC_CACHE["nc"] = build_program()
    nc = _NC_CACHE["nc"]
    maps = make_in_maps(x_prompt, x_sample, rel_bias, w_in_ab, pool_w, pool_scale, w_out_ab, w_in_c, conv_w, w_out_c, ln_g, ln_b)
    res = run_bass_kernel_spmd(nc, maps, core_ids=list(range(NCORES)))
    y_sample = np.zeros((1, S_SAMPLE, D), np.float32)
    y_prompt = np.zeros((4, S_PROMPT, D), np.float32)
    for c in range(NCORES):
        r = res.results[c]
        y_sample[0, 2048 * c:2048 * c + 2048] = r["y0"]
        y_prompt[c // 2, 1024 * (c % 2):1024 * (c % 2) + 1024] = r["y1"]
    return (y_prompt, y_sample)
```
